# Optimizing a Trainium2 kernel written in Bass

```python
import math
import jax, jax.numpy as jnp
from jax import lax
import numpy as np

D_MODEL = 1024
BATCH = 8
SEQ = 2048
DEPTH = 1

N_META = 16
D_MIX = D_MODEL
D_CONV = D_MIX // 2
CONV_HEAD_DIM = 64
N_CONV_HEADS = D_CONV // CONV_HEAD_DIM
D_SSM = D_MIX - D_CONV
SSM_GROUP = 16
N_SSM_GROUPS = D_SSM // SSM_GROUP
SSM_STATE = 64
CONV_WIDTH = 3
D_FF = 2816
D_IN_PROJ = 3 * D_CONV + D_SSM
RMS_EPS = 1e-6
DT_MIN = 1e-3
DT_MAX = 1e-1

kernel_name = "hymba_conv_s5_hybrid_layer"


def rms_norm(x, g):
    xf = x.astype(jnp.float32)
    y = xf * lax.rsqrt(jnp.mean(xf * xf, axis=-1, keepdims=True) + RMS_EPS)
    return (y * g.astype(jnp.float32)).astype(x.dtype)


def causal_dwconv(x, w, b=None):
    c = x.shape[-1]
    y = lax.conv_general_dilated(
        x, w[:, None, :].astype(x.dtype), window_strides=(1,),
        padding=[(CONV_WIDTH - 1, 0)], dimension_numbers=("NWC", "WIO", "NWC"),
        feature_group_count=c)
    if b is not None:
        y = y + b.astype(x.dtype)
    return y


def s5_group_ssm(u, lam_re, lam_im, log_dt, b_re, b_im, c_re, c_im, d_skip, w_glu):
    bsz, seq_len, _ = u.shape
    f32 = jnp.float32
    uf = u.astype(f32).reshape(bsz, seq_len, N_SSM_GROUPS, SSM_GROUP)
    lr = lam_re.astype(f32)
    li = lam_im.astype(f32)
    dt = jnp.exp(log_dt.astype(f32))[:, None]
    mag = jnp.exp(lr * dt)
    ang = li * dt
    a_re = mag * jnp.cos(ang)
    a_im = mag * jnp.sin(ang)
    den = lr * lr + li * li
    nr = a_re - 1.0
    f_re = (nr * lr + a_im * li) / den
    f_im = (a_im * lr - nr * li) / den
    br = b_re.astype(f32)
    bi = b_im.astype(f32)
    bb_re = f_re[..., None] * br - f_im[..., None] * bi
    bb_im = f_re[..., None] * bi + f_im[..., None] * br
    bu_re = jnp.einsum("blgh,gph->blgp", uf, bb_re)
    bu_im = jnp.einsum("blgh,gph->blgp", uf, bb_im)
    a_re_t = jnp.broadcast_to(a_re[None, None], (1, seq_len, N_SSM_GROUPS, SSM_STATE))
    a_im_t = jnp.broadcast_to(a_im[None, None], (1, seq_len, N_SSM_GROUPS, SSM_STATE))

    def combine(e1, e2):
        ar1, ai1, sr1, si1 = e1
        ar2, ai2, sr2, si2 = e2
        return (ar1 * ar2 - ai1 * ai2,
                ar1 * ai2 + ai1 * ar2,
                ar2 * sr1 - ai2 * si1 + sr2,
                ar2 * si1 + ai2 * sr1 + si2)

    _, _, s_re, s_im = lax.associative_scan(combine, (a_re_t, a_im_t, bu_re, bu_im), axis=1)
    y = (jnp.einsum("blgp,ghp->blgh", s_re, c_re.astype(f32))
         - jnp.einsum("blgp,ghp->blgh", s_im, c_im.astype(f32))
         + d_skip.astype(f32) * uf)
    y = y.reshape(bsz, seq_len, D_SSM)
    g = jax.nn.gelu(y)
    out = g * jax.nn.sigmoid(g @ w_glu.astype(f32))
    return out.astype(u.dtype)


def setup_inputs(seed: int = 0) -> dict:
    key = jax.random.key(seed)
    ks = jax.random.split(key, 24)
    f32 = jnp.float32
    nrm = lambda k, shape, s: jax.random.normal(k, shape, f32) * s
    x = jax.random.normal(ks[0], (BATCH, SEQ, D_MODEL), f32)
    meta_tokens = nrm(ks[1], (N_META, D_MODEL), 1.0)
    norm_mix_g = 1.0 + nrm(ks[2], (DEPTH, D_MODEL), 0.02)
    w_in = nrm(ks[3], (DEPTH, D_MODEL, D_IN_PROJ), D_MODEL ** -0.5)
    conv_w = nrm(ks[4], (DEPTH, CONV_WIDTH, D_CONV), CONV_WIDTH ** -0.5)
    n = jnp.arange(SSM_STATE, dtype=f32)
    ssm_lam_re = -0.5 + nrm(ks[5], (DEPTH, N_SSM_GROUPS, SSM_STATE), 1e-3)
    ssm_lam_im = math.pi * n + nrm(ks[6], (DEPTH, N_SSM_GROUPS, SSM_STATE), 1e-3)
    ssm_log_dt = jax.random.uniform(ks[7], (DEPTH, N_SSM_GROUPS), f32,
                                    math.log(DT_MIN), math.log(DT_MAX))
    b_scale = (2.0 * SSM_GROUP) ** -0.5
    ssm_b_re = nrm(ks[8], (DEPTH, N_SSM_GROUPS, SSM_STATE, SSM_GROUP), b_scale)
    ssm_b_im = nrm(ks[9], (DEPTH, N_SSM_GROUPS, SSM_STATE, SSM_GROUP), b_scale)
    c_scale = (2.0 * SSM_STATE) ** -0.5
    ssm_c_re = nrm(ks[10], (DEPTH, N_SSM_GROUPS, SSM_GROUP, SSM_STATE), c_scale)
    ssm_c_im = nrm(ks[11], (DEPTH, N_SSM_GROUPS, SSM_GROUP, SSM_STATE), c_scale)
    ssm_d = nrm(ks[12], (DEPTH, N_SSM_GROUPS, SSM_GROUP), 1.0)
    ssm_w_glu = nrm(ks[13], (DEPTH, D_SSM, D_SSM), D_SSM ** -0.5)
    gain_conv_out = 1.0 + nrm(ks[14], (DEPTH, D_CONV), 0.02)
    gain_ssm_out = 1.0 + nrm(ks[15], (DEPTH, D_SSM), 0.02)
    w_out = nrm(ks[16], (DEPTH, D_MIX, D_MODEL), D_MIX ** -0.5)
    norm_ffn_g = 1.0 + nrm(ks[17], (DEPTH, D_MODEL), 0.02)
    w_up = nrm(ks[18], (DEPTH, D_MODEL, 2 * D_FF), D_MODEL ** -0.5)
    ffn_conv_w = nrm(ks[19], (DEPTH, CONV_WIDTH, 2 * D_FF), CONV_WIDTH ** -0.5)
    ffn_conv_b = nrm(ks[20], (DEPTH, 2 * D_FF), 0.01)
    w_down = nrm(ks[21], (DEPTH, D_FF, D_MODEL), D_FF ** -0.5)
    norm_final_g = 1.0 + nrm(ks[22], (D_MODEL,), 0.02)
    return {"x": x, "meta_tokens": meta_tokens, "norm_mix_g": norm_mix_g, "w_in": w_in,
            "conv_w": conv_w, "ssm_lam_re": ssm_lam_re, "ssm_lam_im": ssm_lam_im,
            "ssm_log_dt": ssm_log_dt, "ssm_b_re": ssm_b_re, "ssm_b_im": ssm_b_im,
            "ssm_c_re": ssm_c_re, "ssm_c_im": ssm_c_im, "ssm_d": ssm_d,
            "ssm_w_glu": ssm_w_glu, "gain_conv_out": gain_conv_out,
            "gain_ssm_out": gain_ssm_out, "w_out": w_out, "norm_ffn_g": norm_ffn_g,
            "w_up": w_up, "ffn_conv_w": ffn_conv_w, "ffn_conv_b": ffn_conv_b,
            "w_down": w_down, "norm_final_g": norm_final_g}


def reference(x, meta_tokens, norm_mix_g, w_in, conv_w, ssm_lam_re, ssm_lam_im, ssm_log_dt,
              ssm_b_re, ssm_b_im, ssm_c_re, ssm_c_im, ssm_d, ssm_w_glu, gain_conv_out,
              gain_ssm_out, w_out, norm_ffn_g, w_up, ffn_conv_w, ffn_conv_b, w_down,
              norm_final_g):
    bsz = x.shape[0]
    meta = jnp.broadcast_to(meta_tokens.astype(x.dtype)[None], (bsz, N_META, D_MODEL))
    h = jnp.concatenate([meta, x], axis=1)
    for i in range(DEPTH):
        hn = rms_norm(h, norm_mix_g[i])
        proj = hn @ w_in[i].astype(h.dtype)
        b_gate = proj[..., :D_CONV]
        c_gate = proj[..., D_CONV:2 * D_CONV]
        v = proj[..., 2 * D_CONV:3 * D_CONV]
        u = proj[..., 3 * D_CONV:]
        conv_out = b_gate * causal_dwconv(c_gate * v, conv_w[i])
        ssm_out = s5_group_ssm(u, ssm_lam_re[i], ssm_lam_im[i], ssm_log_dt[i],
                               ssm_b_re[i], ssm_b_im[i], ssm_c_re[i], ssm_c_im[i],
                               ssm_d[i], ssm_w_glu[i])
        mixed = jnp.concatenate([rms_norm(conv_out, gain_conv_out[i]),
                                 rms_norm(ssm_out, gain_ssm_out[i])], axis=-1)
        h = h + mixed @ w_out[i].astype(h.dtype)
        hn = rms_norm(h, norm_ffn_g[i])
        up = causal_dwconv(hn @ w_up[i].astype(h.dtype), ffn_conv_w[i], ffn_conv_b[i])
        a = up[..., :D_FF]
        val = up[..., D_FF:]
        h = h + (jax.nn.silu(a) * val) @ w_down[i].astype(h.dtype)
    y = rms_norm(h, norm_final_g)
    return y[:, N_META:]
```

```python
import math
OPT_SSM = True
OPT_PASS = True
OPT_FFN = True
OPT_FFN2 = True
OPT_TAB = True
OPT_MAT = True
OPT_MATQ = True
OPT_MATR = True
from contextlib import ExitStack

import numpy as np
import concourse.bass as bass
import concourse.mybir as mybir
from concourse.bass_utils import run_bass_kernel_spmd

F32 = mybir.dt.float32
BF16 = mybir.dt.bfloat16
ALU = mybir.AluOpType
AF = mybir.ActivationFunctionType

D = 1024
L = 2064
NMETA = 16
LP = L + 2
KD = 8
DFF = 2816
NF = 22
NCH = 258
EPS = 1e-6
TILES = [(416 * i, min(416, L - 416 * i)) for i in range(5)]
TWO_PI_S = 2.0 * math.pi * (1.0 - 2e-6)
MAGIC = 12582912.0
JV = [7, 6, 5, 4, 3, 2, 1, 0, 1, 2, 3, 4, 5, 6, 7, 8, -7, -6, -5, -4, -3, -2, -1, 0]
FGROUPS = [list(range(0, 8)), list(range(8, 15)), list(range(15, 22))]


class Buf:
    __slots__ = ("name", "lws", "rd")

    def __init__(self, name=""):
        self.name = name
        self.lws = []
        self.rd = []


class Op:
    __slots__ = ("eng", "fn", "deps", "sig", "idx", "ev", "dma")

    def __init__(self, eng, fn, sig, dma):
        self.eng = eng
        self.fn = fn
        self.deps = []
        self.sig = sig
        self.idx = None
        self.ev = None
        self.dma = dma


class Prog:
    ENGS = ("pe", "act", "dve", "pool", "sp")

    def __init__(self, nc, ndma_sems=24):
        self.nc = nc
        self.ops = {e: [] for e in self.ENGS}
        self.sems = {}
        self.ndma = ndma_sems
        self.dma_ring = {}
        self.dma_cnt = {}
        self.dma_last = {}
        self.dma_i = {}
        self.bar = []
        self.all_dmas = []

    def open(self, stack):
        nc = self.nc
        for e in self.ENGS:
            self.sems[e] = stack.enter_context(nc.semaphore("s_" + e))
        for q in ("sp", "act", "pool"):
            n = self.ndma if q == "sp" else 8
            self.dma_ring[q] = [stack.enter_context(nc.semaphore(f"d_{q}{i}")) for i in range(n)]
            self.dma_cnt[q] = [0] * n
            self.dma_last[q] = [None] * n
            self.dma_i[q] = 0

    def op(self, eng, fn, reads=(), writes=(), sig=True, join=False):
        o = Op(eng, fn, sig, None)
        o.deps.extend(self.bar)
        self._deps(o, reads, writes, join)
        o.idx = len(self.ops[eng])
        self.ops[eng].append(o)
        return o

    def dma(self, q, fn, reads=(), writes=(), join=False):
        n = len(self.dma_ring[q])
        i = self.dma_i[q]
        self.dma_i[q] = (i + 1) % n
        o = Op(q, fn, False, (q, i))
        o.deps.extend(self.bar)
        prev = self.dma_last[q][i]
        if prev is not None:
            o.deps.append(prev)
        self._deps(o, reads, writes, join)
        self.dma_cnt[q][i] += 16
        o.ev = (self.dma_ring[q][i], self.dma_cnt[q][i])
        self.dma_last[q][i] = o
        o.idx = len(self.ops[q])
        self.ops[q].append(o)
        self.all_dmas.append(o)
        return o

    def _deps(self, o, reads, writes, join=False):
        for r in reads:
            o.deps.extend(r.lws)
        for w in writes:
            if join and w.lws and not w.rd:
                continue
            o.deps.extend(w.lws)
            o.deps.extend(w.rd)
        for r in reads:
            r.rd.append(o)
        for w in writes:
            if join and w.lws and not w.rd:
                w.lws.append(o)
            else:
                w.lws = [o]
                w.rd = []

    def barrier(self):
        deps = []
        for e in self.ENGS:
            comp = [o for o in self.ops[e] if o.dma is None]
            if comp:
                comp[-1].sig = True
                deps.append(comp[-1])
        deps.extend(self.all_dmas)
        self.all_dmas = []
        self.bar = deps

    def finalize(self):
        for e in self.ENGS:
            lst = [o for o in self.ops[e] if o.dma is None]
            if lst:
                lst[-1].sig = True
            cnt = 0
            pending = []
            for o in lst:
                pending.append(o)
                if o.sig:
                    cnt += 1
                    for p in pending:
                        p.ev = (self.sems[e], cnt)
                    pending = []

    def emit(self, block):
        self.finalize()

        def run(e):
            def body(eng):
                waited = {}
                for o in self.ops[e]:
                    need = {}
                    for d in o.deps:
                        if d.dma is None and d.eng == e:
                            if e == "pe" and o.dma is None:
                                continue
                            if not d.sig:
                                nxt = [x for x in self.ops[e][d.idx:] if x.dma is None and x.sig][0]
                                if nxt.idx >= o.idx:
                                    continue
                        s, v = d.ev
                        if need.get(s.name, (None, 0))[1] < v:
                            need[s.name] = (s, v)
                    for sn, (s, v) in need.items():
                        if waited.get(sn, 0) < v:
                            eng.wait_ge(s, v)
                            waited[sn] = v
                    ins = o.fn(eng)
                    if ins is None:
                        continue
                    if o.dma is not None:
                        ins.then_inc(o.ev[0], 16)
                    elif o.sig:
                        ins.then_inc(o.ev[0], 1)
            return body

        block.tensor(run("pe"))
        block.scalar(run("act"))
        block.vector(run("dve"))
        block.gpsimd(run("pool"))
        block.sync(run("sp"))


class Rot:
    def __init__(self, items):
        self.items = items
        self.i = 0

    def get(self):
        it = self.items[self.i]
        self.i = (self.i + 1) % len(self.items)
        return it


def _build(dbg=False):
    nc = bass.Bass("TRN2", target_bir_lowering=False)

    def din(name, shape):
        return nc.dram_tensor(name, list(shape), F32, kind="ExternalInput").ap()

    xT_d = din("xT", [128, KD, LP])
    w_in_d = din("w_in", [128, KD, 2048])
    w_out_d = din("w_out", [128, KD, 1024])
    w_glu_d = din("w_glu", [128, 4, 512])
    w_up_d = din("w_up", [NF, 128, KD, 256])
    w_dn_d = din("w_dn", [NF, 128, 1024])
    small_specs = dict(gmix=8, gffn=8, gfin=8, gainc=4, gains=4, cw=12, fwa=66, fwv=66, fba=22, fbv=22,
                       lr=32, li=32, ldt=32, X1=512, X2=512, CX1=512, CX2=512, Dv=32,
                       ident=128, mask=128, cidx=NCH, jv=24, sgn=1)
    small_d = {k: din(k, [128, n]) for k, n in small_specs.items()}
    yT_d = nc.dram_tensor("yT", [128, KD, 2048], F32, kind="ExternalOutput").ap()
    scrU = nc.dram_tensor("scrU", [4, 128, 8, NCH], BF16).ap()
    scrG = nc.dram_tensor("scrG", [4, 128, 8, NCH], BF16).ap()
    dbg_d = {}
    if dbg:
        dbg_d["h_mix"] = nc.dram_tensor("h_mix", [128, KD, LP], F32, kind="ExternalOutput").ap()
        dbg_d["gT"] = nc.dram_tensor("gT", [128, 4 * 8 * NCH], BF16, kind="ExternalOutput").ap()
        dbg_d["uT"] = nc.dram_tensor("uT", [128, 4 * 8 * NCH], BF16, kind="ExternalOutput").ap()
        dbg_d["pr"] = nc.dram_tensor("pr", [128, 32 * 24], F32, kind="ExternalOutput").ap()
        dbg_d["pi"] = nc.dram_tensor("pi", [128, 32 * 24], F32, kind="ExternalOutput").ap()

    with ExitStack() as st:
        def sbt(name, shape, dt):
            return st.enter_context(nc.sbuf_tensor(name, list(shape), dt))

        hT = sbt("hT", [128, KD, LP], F32)
        sm = {k: sbt("sm_" + k, [128, n], F32) for k, n in small_specs.items()}
        ones_bf = sbt("ones_bf", [128, 128], BF16)
        tiny = sbt("tiny", [128, 16, 32], F32)

        ARENA_BYTES = 128 * 1024
        arena = sbt("arena", [128, ARENA_BYTES // 2], BF16)

        class Carver:
            def __init__(self):
                self.off = 0

            def take(self, shape, dt):
                n = int(np.prod(shape))
                nb = n * (4 if dt == F32 else 2)
                nb = (nb + 63) // 64 * 64
                assert self.off + nb <= ARENA_BYTES, (self.off, nb)
                v = arena[:, self.off // 2:(self.off + nb) // 2]
                if dt == F32:
                    v = v.bitcast(F32)
                v = v[:, 0:n]
                if len(shape) == 2:
                    v = v.rearrange("p (a b) -> p a b", a=shape[0], b=shape[1])
                elif len(shape) == 3:
                    v = v.rearrange("p (a b c) -> p a b c", a=shape[0], b=shape[1], c=shape[2])
                self.off += nb
                return v

        cm = Carver()
        w_in = cm.take([KD, 2048], BF16)
        xbf = cm.take([KD, 418], BF16)
        sqb = cm.take([KD, 418], BF16)
        regionA_end = cm.off
        w_outh = cm.take([4, 1024], BF16)
        w_glu = cm.take([4, 512], BF16)
        ugT = cm.take([4, 8, NCH], BF16)
        rstd = cm.take([418], F32)
        sqrt_t = cm.take([418], F32)
        tcb2 = [cm.take([418], F32) for _ in range(2)]
        cvb2 = [cm.take([418], F32) for _ in range(2)]
        tb2 = [cm.take([416], F32) for _ in range(2)]
        co = cm.take([4, 416], F32)
        cobf = cm.take([4, 416], BF16)
        sgb = cm.take([416], F32)
        cobf_b = cm.take([4, 416], BF16)
        cs_ = Carver()
        UG = cs_.take([32, NCH], BF16)
        Wf = [cs_.take([128], F32) for _ in range(2)]
        Qf = [cs_.take([128], F32) for _ in range(2)]
        sT2 = [cs_.take([128], F32) for _ in range(2)]
        L1 = [cs_.take([128], BF16) for _ in range(2)]
        L2 = [cs_.take([128], BF16) for _ in range(2)]
        Toe = [cs_.take([128], BF16) for _ in range(3)]
        R1 = [cs_.take([128], BF16) for _ in range(3)]
        R2 = [cs_.take([128], BF16) for _ in range(3)]
        toet = [cs_.take([128], F32) for _ in range(2)]
        pA = [cs_.take([128], F32) for _ in range(2)]
        pB = [cs_.take([128], F32) for _ in range(2)]
        ES = [cs_.take([2, NCH], F32) for _ in range(2)]
        EC = [cs_.take([2, NCH], F32) for _ in range(2)]
        PH = cs_.take([2, NCH], F32)
        PT = cs_.take([2, NCH], F32)
        M1 = [cs_.take([NCH], F32) for _ in range(2)]
        M2 = [cs_.take([NCH], F32) for _ in range(2)]
        X1b = [cs_.take([NCH + 2], BF16) for _ in range(2)]
        X2b = [cs_.take([NCH + 2], BF16) for _ in range(2)]
        assert cs_.off <= regionA_end, (cs_.off, regionA_end)
        ssm_param_start = cm.off
        T1h = [cm.take([16, 24], F32) for _ in range(2)]
        T2h = [cm.take([16, 24], F32) for _ in range(2)]
        TB1 = cm.take([32, 16], F32)
        TB2 = cm.take([32, 16], F32)
        PRa = cm.take([32, 24], F32)
        PIa = cm.take([32, 24], F32)
        BA = cm.take([32, 16], F32)
        SBB = cm.take([32, 16], F32)
        SCX1 = cm.take([32, 16], F32)
        NSCX1 = cm.take([32, 16], F32)
        NCX2 = cm.take([32, 16], F32)
        mixer_end = cm.off
        cf = Carver()
        hnT = cf.take([KD, LP], BF16)
        assert cf.off <= regionA_end + 8 * 1024 or True
        hid = [cf.take([L], BF16) for _ in range(10)]
        wdn = [cf.take([1024], BF16) for _ in range(10)]
        a0b = [cf.take([416], F32) for _ in range(3)]
        v0b = [cf.take([416], F32) for _ in range(3)]
        sab = [a0b[2], v0b[2]]
        sqh = cf.take([KD, 416], BF16)
        rstd2 = cf.take([416], F32)
        sqrt2 = cf.take([416], F32)
        assert cf.off >= ssm_param_start, (cf.off, ssm_param_start)
        wup = cf.take([3, KD, 256], BF16)
        assert 2 * KD * LP <= 32768 + 2 * KD * 418

        banks = [st.enter_context(nc.psum_tensor(f"ps{i}", [128, 512], F32)) for i in range(8)]
        bbuf = [Buf(f"ps{i}") for i in range(8)]

        P = Prog(nc)
        P.open(st)
        block = st.enter_context(nc.Block())

        def TT(eng, out, in0, in1, op, r, w, sig=True):
            return P.op(eng, lambda e: e.tensor_tensor(out=out, in0=in0, in1=in1, op=op), r, w, sig)

        def TS(eng, out, in0, s1, s2, op0, op1, r, w):
            if s2 is None:
                return P.op(eng, lambda e: e.tensor_scalar(out=out, in0=in0, scalar1=s1, scalar2=None, op0=op0), r, w)
            return P.op(eng, lambda e: e.tensor_scalar(out=out, in0=in0, scalar1=s1, scalar2=s2, op0=op0, op1=op1), r, w)

        def STT(eng, out, in0, sc, in1, op0, op1, r, w):
            return P.op(eng, lambda e: e.scalar_tensor_tensor(out=out, in0=in0, scalar=sc, in1=in1, op0=op0, op1=op1), r, w)

        def ACTF(out, in_, func, r, w, bias=0.0, scale=1.0):
            return P.op("act", lambda e: e.activation(out=out, in_=in_, func=func, bias=bias, scale=scale), r, w)

        def MM(out, lhsT, rhs, start, stop, r, w, sig=False):
            return P.op("pe", lambda e: e.matmul(out, lhsT=lhsT, rhs=rhs, start=start, stop=stop), r, w, sig)

        def DMA(q, out, in_, r, w, join=False):
            return P.dma(q, lambda e: e.dma_start(out=out, in_=in_), r, w, join)

        def round_sub(eng, x, tmp, bx, bt):
            TS(eng, tmp, x, MAGIC, None, ALU.add, None, [bx], [bt])
            TS(eng, tmp, tmp, -MAGIC, None, ALU.add, None, [bt], [bt])
            TT(eng, x, x, tmp, ALU.subtract, [bx, bt], [bx])

        B_h = [Buf(f"h{i}") for i in range(5)]
        B_sm = {k: Buf("sm_" + k) for k in small_specs}
        B_ones = Buf("ones")
        B_win = Buf("w_in")
        B_winj = [Buf(f"w_in{j}") for j in range(4)]
        B_wouth = Buf("w_outh")
        B_wglu = Buf("w_glu")
        B_xbf = Buf("xbf")
        B_sqb = Buf("sqb")
        B_rstd = Buf("rstd")
        B_sqrt = Buf("sqrt")
        B_ug = [Buf(f"ugT{g}") for g in range(32)]
        B_UG = [Buf(f"UG{g}") for g in range(32)]
        B_tc2, B_cv2, B_tb2 = [Buf(), Buf()], [Buf(), Buf()], [Buf(), Buf()]
        B_co, B_cobf, B_sg = Buf(), Buf(), Buf()
        B_PR, B_PI, B_BA, B_SBB, B_SCX1, B_tiny = Buf(), Buf(), Buf(), Buf(), Buf(), Buf()
        B_T1h, B_T2h, B_TB1, B_TB2 = [Buf(), Buf()], [Buf(), Buf()], Buf(), Buf()

        for k, ap in small_d.items():
            DMA("sp", sm[k][:], ap, [], [B_sm[k]])
        w_in4 = w_in.rearrange("p k (b c) -> p k b c", b=4)
        w_in_d4 = w_in_d.rearrange("p k (b c) -> p k b c", b=4)
        for j in range(4):
            DMA("pool", w_in4[:, :, :, 128 * j:128 * (j + 1)], w_in_d4[:, :, :, 128 * j:128 * (j + 1)], [], [B_winj[j]])
        for ti, (s0, V) in enumerate(TILES):
            lo = 0 if ti == 0 else s0 + 2
            DMA("sp", hT[:, :, lo:s0 + 2 + V], xT_d[:, :, lo:s0 + 2 + V], [B_win] if ti >= 1 else [], [B_h[ti]])
        P.op("dve", lambda e: e.memset(ones_bf[:], 1.0), [], [B_ones])

        tn = lambda i: tiny[:, i, :]
        lr, li, ldt = sm["lr"][:], sm["li"][:], sm["ldt"][:]
        sgn = sm["sgn"][:, 0:1]
        t_dt, t_y8, t_th, t_r8, t_f8, t_nr, t_den, t_fr, t_fi, t_a, t_b = [tn(i) for i in range(11)]
        bt = B_tiny
        X1v = sm["X1"][:].rearrange("p (g h) -> p g h", g=32)
        X2v = sm["X2"][:].rearrange("p (g h) -> p g h", g=32)
        CX1v = sm["CX1"][:].rearrange("p (g h) -> p g h", g=32)
        CX2v = sm["CX2"][:].rearrange("p (g h) -> p g h", g=32)

        B_den = Buf()

        def ssm_params_part0():
            TT("pool", t_den, lr, lr, ALU.mult, [B_sm["lr"]], [B_den])
            TT("pool", t_b, li, li, ALU.mult, [B_sm["li"]], [B_den])
            TT("pool", t_den, t_den, t_b, ALU.add, [B_den], [B_den])
            P.op("dve", lambda e: e.reciprocal(out=t_den, in_=t_den), [B_den], [B_den])
            ACTF(t_dt, ldt, AF.Exp, [B_sm["ldt"]], [bt])
            TT("pool", t_y8, lr, t_dt, ALU.mult, [B_sm["lr"], bt], [bt])
            TT("pool", t_th, li, t_dt, ALU.mult, [B_sm["li"], bt], [bt])
            TS("pool", t_th, t_th, 1.0 / (2.0 * math.pi), None, ALU.mult, None, [bt], [bt])
            jv3 = sm["jv"][:].unsqueeze(1).to_broadcast([128, 16, 24])
            for half in range(2):
                gs = slice(16 * half, 16 * half + 16)
                T1, T2, B_T1, B_T2 = T1h[half], T2h[half], B_T1h[half], B_T2h[half]
                y8b = t_y8[:, gs].unsqueeze(2).to_broadcast([128, 16, 24])
                thb = t_th[:, gs].unsqueeze(2).to_broadcast([128, 16, 24])
                TT("pool", T1, jv3, y8b, ALU.mult, [B_sm["jv"], bt], [B_T1])
                TS("pool", T2, T1, 1.0 / 6.0, 1.0, ALU.mult, ALU.add, [B_T1], [B_T2])
                for kk in (5, 4, 3, 2, 1):
                    TT("pool", T2, T2, T1, ALU.mult, [B_T2, B_T1], [B_T2])
                    TS("pool", T2, T2, 1.0 / kk, 1.0, ALU.mult, ALU.add, [B_T2], [B_T2])
                TT("pool", T1, jv3, thb, ALU.mult, [B_sm["jv"], bt], [B_T1])
                round_sub("pool", T1, PRa[:, gs, :], B_T1, B_PR)
                P.op("pool", (lambda gs, T2: lambda e: e.tensor_copy(out=t_r8[:, gs], in_=T2[:, :, 15]))(gs, T2), [B_T2], [bt])
            TS("pool", t_f8, t_th, 8.0, None, ALU.mult, None, [bt], [bt])
            round_sub("pool", t_f8, t_a, bt, bt)
            TS("pool", SCX1, CX1v, sgn, None, ALU.mult, None, [B_sm["CX1"], B_sm["sgn"]], [B_SCX1])
            TS("pool", NSCX1, SCX1, -1.0, None, ALU.mult, None, [B_SCX1], [B_SCX1])
            TS("pool", NCX2, CX2v, -1.0, None, ALU.mult, None, [B_sm["CX2"]], [B_SCX1])

        def ssm_params_part1():
            for half in range(2):
                gs = slice(16 * half, 16 * half + 16)
                T1, T2, B_T1, B_T2 = T1h[half], T2h[half], B_T1h[half], B_T2h[half]
                ACTF(PIa[:, gs, :], T1, AF.Sin, [B_T1], [B_PI], scale=TWO_PI_S)
                ACTF(T1, T1, AF.Abs, [B_T1], [B_T1])
                ACTF(PRa[:, gs, :], T1, AF.Sin, [B_T1], [B_PR], scale=-TWO_PI_S, bias=TWO_PI_S / 4.0)
                TT("pool", PIa[:, gs, :], PIa[:, gs, :], T2, ALU.mult, [B_PI, B_T2], [B_PI])
                TT("pool", PRa[:, gs, :], PRa[:, gs, :], T2, ALU.mult, [B_PR, B_T2], [B_PR])
            ar = PRa[:, :, 6]
            ai = PIa[:, :, 6]
            TS("pool", t_nr, ar, -1.0, None, ALU.add, None, [B_PR], [bt])
            TT("pool", t_fr, t_nr, lr, ALU.mult, [bt, B_sm["lr"]], [bt])
            TT("pool", t_a, ai, li, ALU.mult, [B_PI, B_sm["li"]], [bt])
            TT("pool", t_fr, t_fr, t_a, ALU.add, [bt], [bt])
            TT("pool", t_fr, t_fr, t_den, ALU.mult, [bt, B_den], [bt])
            TT("pool", t_fi, ai, lr, ALU.mult, [B_PI, B_sm["lr"]], [bt])
            TT("pool", t_a, t_nr, li, ALU.mult, [bt, B_sm["li"]], [bt])
            TT("pool", t_fi, t_fi, t_a, ALU.subtract, [bt], [bt])
            TT("pool", t_fi, t_fi, t_den, ALU.mult, [bt, B_den], [bt])
            TS("pool", t_fi, t_fi, sgn, None, ALU.mult, None, [bt, B_sm["sgn"]], [bt])
            frb = t_fr.unsqueeze(2).to_broadcast([128, 32, 16])
            fib = t_fi.unsqueeze(2).to_broadcast([128, 32, 16])
            TT("pool", TB1, frb, X1v, ALU.mult, [bt, B_sm["X1"]], [B_TB1])
            TT("pool", TB2, fib, X2v, ALU.mult, [bt, B_sm["X2"]], [B_TB2])
            TT("pool", BA, TB1, TB2, ALU.add, [B_TB1, B_TB2], [B_BA])
            TT("pool", TB1, frb, X2v, ALU.mult, [bt, B_sm["X2"]], [B_TB1])
            TT("pool", TB2, fib, X1v, ALU.mult, [bt, B_sm["X1"]], [B_TB2])
            TT("pool", SBB, TB1, TB2, ALU.subtract, [B_TB1, B_TB2], [B_SBB])
            TS("pool", SBB, SBB, sgn, None, ALU.mult, None, [B_SBB, B_sm["sgn"]], [B_SBB])

        ssm_params_part0()
        DMA("pool", w_outh, w_out_d[:, 0:4, :], [], [B_wouth])
        DMA("pool", w_glu, w_glu_d, [], [B_wglu])

        ps_rot = Rot([(banks[i], bbuf[i]) for i in range(6)])
        ps_stat = (banks[6], bbuf[6])
        ps_acc = Rot([(banks[7], bbuf[7]), (banks[6], bbuf[6])])
        gmix, gffn, gfin = sm["gmix"], sm["gffn"], sm["gfin"]
        cwv = sm["cw"][:].rearrange("p (t j) -> p t j", t=3)

        def rms_stats(n, nk, src_sq, bsq, scale, out_rstd, out_sqrt, brstd, bsqrt, pss=None):
            ps, bps = pss if pss is not None else ps_stat
            for k in range(nk):
                MM(ps[:, 0:n], ones_bf[:], src_sq(k), k == 0, k == nk - 1, [B_ones, bsq], [bps], sig=(k == nk - 1))
            ACTF(out_sqrt[:, 0:n], ps[:, 0:n], AF.Ln, [bps], [bsqrt], bias=EPS, scale=scale)
            ACTF(out_rstd[:, 0:n], out_sqrt[:, 0:n], AF.Exp, [bsqrt], [brstd], scale=-0.5)

        def pass_prologue_sq(ti):
            s0, V = TILES[ti]
            bh = B_h[ti]
            ACTF(sqb[:, :, 0:V], hT[:, :, s0 + 2:s0 + 2 + V], AF.Square, [bh], [B_sqb])

        def pass_prologue_mm(ti):
            s0, V = TILES[ti]
            rms_stats(V, KD, lambda k: sqb[:, k, 0:V], B_sqb, 1.0 / D, rstd, rstd, B_rstd, B_rstd)

        def pass_prologue_stats(ti):
            pass_prologue_sq(ti)
            pass_prologue_mm(ti)

        def pass_prologue_xbf(ti):
            s0, V = TILES[ti]
            N = V + 2
            bh = B_h[ti]
            if ti == 0:
                P.op("dve", lambda e: e.memset(xbf[:, :, 0:2], 0.0), [], [B_xbf])
            else:
                Np = TILES[ti - 1][1] + 2
                P.op("dve", (lambda Np: lambda e: e.tensor_copy(out=xbf[:, :, 0:2], in_=xbf[:, :, Np - 2:Np]))(Np), [B_xbf], [B_xbf])
            for k in range(KD):
                STT("dve", xbf[:, k, 2:N], hT[:, k, s0 + 2:s0 + 2 + V], gmix[:, k:k + 1], rstd[:, 0:V],
                    ALU.mult, ALU.mult, [bh, B_sm["gmix"], B_rstd], [B_xbf])

        def pass_unit(ti, j):
            s0, V = TILES[ti]
            N = V + 2
            c0, c1 = s0 // 8, (s0 + V) // 8
            if True:
                ps, bps = ps_rot.get()
                for k in range(KD):
                    MM(ps[:, 0:N], w_in[:, k, 1536 + 128 * j:1536 + 128 * (j + 1)], xbf[:, k, 0:N], k == 0, k == KD - 1,
                       [B_winj[j], B_xbf], [bps], sig=(k == KD - 1))
                pu, bpu = ps, bps
                psc, bpsc = ps_rot.get()
                psv, bpsv = ps_rot.get()
                psb, bpsb = ps_rot.get()
                for (ps, bps, col) in ((psc, bpsc, 512 + 128 * j), (psv, bpsv, 1024 + 128 * j), (psb, bpsb, 128 * j)):
                    for k in range(KD):
                        MM(ps[:, 0:N], w_in[:, k, col:col + 128], xbf[:, k, 0:N], k == 0, k == KD - 1,
                           [B_winj[j], B_xbf], [bps], sig=(k == KD - 1))
                jj = j % 2
                tcb, cvb, tb = tcb2[jj], cvb2[jj], tb2[jj]
                B_tc, B_cv, B_tb = B_tc2[jj], B_cv2[jj], B_tb2[jj]
                ACTF(tcb[:, 0:N], psc[:, 0:N], AF.Copy, [bpsc], [B_tc])
                TT("dve", cvb[:, 0:N], psv[:, 0:N], tcb[:, 0:N], ALU.mult, [bpsv, B_tc], [B_cv])
                ACTF(tb[:, 0:V], cvb[:, 2:N], AF.Copy, [B_cv, B_sm["cw"]], [B_tb], scale=cwv[:, 2, j:j + 1])
                ACTF(ugT[:, j, :, c0:c1].rearrange("p k c -> p c k"), pu[:, 2:N].rearrange("p (c k) -> p c k", k=8),
                     AF.Copy, [bpu], B_ug[8 * j:8 * j + 8])
                STT("dve", tb[:, 0:V], cvb[:, 1:N - 1], cwv[:, 1, j:j + 1], tb[:, 0:V], ALU.mult, ALU.add, [B_cv, B_tb, B_sm["cw"]], [B_tb])
                STT("dve", tb[:, 0:V], cvb[:, 0:N - 2], cwv[:, 0, j:j + 1], tb[:, 0:V], ALU.mult, ALU.add, [B_cv, B_tb, B_sm["cw"]], [B_tb])
                TT("dve", co[:, j, 0:V], psb[:, 2:N], tb[:, 0:V], ALU.mult, [bpsb, B_tb], [B_co])

        def pass_tail_a_stats(ti):
            s0, V = TILES[ti]
            ACTF(sqb[:, 0:4, 0:V], co[:, :, 0:V], AF.Square, [B_co], [B_sqb])
            rms_stats(V, 4, lambda k: sqb[:, k, 0:V], B_sqb, 1.0 / 512, sqrt_t, sqrt_t, B_sqrt, B_sqrt)

        def pass_tail_a_cobf(ti):
            s0, V = TILES[ti]
            for j in range(4):
                STT("dve", cobf[:, j, 0:V], co[:, j, 0:V], sm["gainc"][:, j:j + 1], sqrt_t[:, 0:V],
                    ALU.mult, ALU.mult, [B_co, B_sm["gainc"], B_sqrt], [B_cobf])

        def pass_tail_b(ti):
            s0, V = TILES[ti]
            bh = B_h[ti]
            for o in range(KD):
                ps, bps = ps_acc.get()
                for k in range(4):
                    MM(ps[:, 0:V], w_outh[:, k, 128 * o:128 * (o + 1)], cobf[:, k, 0:V], k == 0, k == 3,
                       [B_wouth, B_cobf], [bps], sig=(k == 3))
                TT("dve", hT[:, o, s0 + 2:s0 + 2 + V], hT[:, o, s0 + 2:s0 + 2 + V], ps[:, 0:V], ALU.add, [bps, bh], [bh])

        pass_prologue_stats(0)
        pass_prologue_xbf(0)
        for ti in range(5):
            for j in range(4):
                pass_unit(ti, j)
                if j == 0 and ti + 1 < 5:
                    pass_prologue_sq(ti + 1)
                if j == 1:
                    if ti > 0:
                        pass_tail_b(ti - 1)
                    if ti == 1:
                        ssm_params_part1()
                if j == 2 and ti + 1 < 5:
                    pass_prologue_mm(ti + 1)
            pass_tail_a_stats(ti)
            if ti + 1 < 5:
                pass_prologue_xbf(ti + 1)
            pass_tail_a_cobf(ti)
        pass_tail_b(4)

        if dbg:
            for j in range(4):
                DMA("sp", dbg_d["uT"][:, j * 8 * NCH:(j + 1) * 8 * NCH], ugT[:, j, :, :].rearrange("p k c -> p (k c)"),
                    B_ug[8 * j:8 * j + 8], [])
            DMA("sp", dbg_d["pr"], PRa.rearrange("p g j -> p (g j)"), [B_PR], [])
            DMA("sp", dbg_d["pi"], PIa.rearrange("p g j -> p (g j)"), [B_PI], [])

        P.barrier()
        DMA("pool", w_outh, w_out_d[:, 4:8, :], [], [B_wouth])

        B_slot = {n: [Buf(), Buf()] for n in "Wf Qf sT2 L1 L2 toet M1 M2 X1 X2 pA pB".split()}
        B_slot3 = {n: [Buf(), Buf(), Buf()] for n in "Toe R1 R2".split()}
        B_ES, B_EC, B_PH, B_PT = [Buf(), Buf()], [Buf(), Buf()], Buf(), Buf()
        psS = [(banks[0], bbuf[0]), (banks[1], bbuf[1])]
        psS1 = [(banks[2], bbuf[2]), (banks[3], bbuf[3])]
        psS2 = [(banks[4], bbuf[4]), (banks[5], bbuf[5])]
        psY = [(banks[6], bbuf[6]), (banks[7], bbuf[7])]
        for s in range(2):
            P.op("pool", (lambda s: lambda e: e.memset(X1b[s][:, 0:1], 0.0))(s), [], [B_slot["X1"][s]])
            P.op("pool", (lambda s: lambda e: e.memset(X2b[s][:, 0:1], 0.0))(s), [], [B_slot["X2"][s]])
        B_scrU = [Buf() for _ in range(4)]
        B_scrG = [Buf() for _ in range(4)]
        for j in range(4):
            DMA("sp", scrU[j], ugT[:, j, :, :], B_ug[8 * j:8 * j + 8], [B_scrU[j]])
        for j in range(4):
            for kp in range(8):
                src = scrU[j, :, kp, :].rearrange("(g h) c -> h g c", h=16)
                DMA("sp", UG[16 * kp:16 * kp + 16, 8 * j:8 * j + 8, :], src, [B_scrU[j]], B_UG[8 * j:8 * j + 8], join=(kp > 0))
        identv, maskv = sm["ident"][:], sm["mask"][:]
        SE = "dve" if OPT_SSM else "pool"
        cidx3 = sm["cidx"][:].unsqueeze(1).to_broadcast([128, 2, NCH])
        W3 = lambda t: t.rearrange("p (a b) -> p a b", a=8)
        tab_slot = {}

        def slots(g):
            s, s3 = g % 2, g % 3
            bs = {n: B_slot[n][s] for n in B_slot}
            bs.update({n: B_slot3[n][s3] for n in B_slot3})
            return s, s3, bs

        def tables_act1(g):
            tsl = (g // 2) % 2
            tab_slot[g] = tab_slot[g + 1] = tsl
            for q in range(2):
                ACTF(PH[:, q, :], sm["cidx"][:], AF.Copy, [B_sm["cidx"], bt], [B_PH], scale=t_f8[:, g + q:g + q + 1])
            ACTF(PT, PH, AF.Identity, [B_PH], [B_PT], bias=MAGIC)
            ACTF(PT, PT, AF.Identity, [B_PT], [B_PT], bias=-MAGIC)

        def tables_dve(g):
            TT("dve", PH, PH, PT, ALU.subtract, [B_PH, B_PT], [B_PH])

        def tables_act(g):
            tsl = tab_slot[g]
            ACTF(ES[tsl], PH, AF.Sin, [B_PH], [B_ES[tsl]], scale=TWO_PI_S)
            ACTF(PT, PH, AF.Abs, [B_PH], [B_PT])
            ACTF(EC[tsl], PT, AF.Sin, [B_PT], [B_EC[tsl]], scale=-TWO_PI_S, bias=TWO_PI_S / 4.0)

        def bc_k(t, g, lo):
            return t[:, g, lo:lo + 8].unsqueeze(2).to_broadcast([128, 8, 16])

        def bc_h(t, g):
            return t[:, g, :].unsqueeze(1).to_broadcast([128, 8, 16])

        def setup_dve_wq(g):
            s, s3, bs = slots(g)
            TT("dve", W3(Wf[s]), bc_k(PRa, g, 0), bc_h(BA, g), ALU.mult, [B_PR, B_BA], [bs["Wf"]])
            TT("dve", W3(Qf[s]), bc_k(PRa, g, 16), bc_h(NSCX1, g), ALU.mult, [B_PR, B_SCX1], [bs["Qf"]])
            TT("dve", W3(sT2[s]), bc_k(PIa, g, 0), bc_h(SBB, g), ALU.mult, [B_PI, B_SBB], [bs["sT2"]])
            TT("dve", W3(toet[s]), bc_k(PIa, g, 16), bc_h(NCX2, g), ALU.mult, [B_PI, B_SCX1], [bs["toet"]])
            TT("dve", Wf[s], Wf[s], sT2[s], ALU.add, [bs["Wf"], bs["sT2"]], [bs["Wf"]])
            TT("dve", Qf[s], Qf[s], toet[s], ALU.add, [bs["Qf"], bs["toet"]], [bs["Qf"]])

        def setup_pool_r(g):
            s, s3, bs = slots(g)
            TT("pool", W3(pA[s]), bc_k(PRa, g, 8), bc_h(NSCX1, g), ALU.mult, [B_PR, B_SCX1], [bs["pA"]])
            TT("pool", W3(pB[s]), bc_k(PIa, g, 8), bc_h(NCX2, g), ALU.mult, [B_PI, B_SCX1], [bs["pB"]])
            TT("pool", R1[s3], pA[s], pB[s], ALU.add, [bs["pA"], bs["pB"]], [bs["R1"]])
            TT("pool", W3(pA[s]), bc_k(PIa, g, 8), bc_h(SCX1, g), ALU.mult, [B_PI, B_SCX1], [bs["pA"]])
            TT("pool", W3(pB[s]), bc_k(PRa, g, 8), bc_h(NCX2, g), ALU.mult, [B_PR, B_SCX1], [bs["pB"]])
            TT("pool", R2[s3], pA[s], pB[s], ALU.add, [bs["pA"], bs["pB"]], [bs["R2"]])

        def setup_pe(g):
            s, s3, bs = slots(g)
            pS, bpS = psS[s]
            P.op("pe", (lambda pS, s: lambda e: e.transpose(pS[:, 0:128], Wf[s], identv))(pS, s), [bs["Wf"], B_sm["ident"]], [bpS], sig=False)
            MM(pS[:, 128:256], Wf[s], Qf[s], True, True, [bs["Wf"], bs["Qf"]], [bpS], sig=True)

        def setup_act_l(g):
            s, s3, bs = slots(g)
            pS, bpS = psS[s]
            ACTF(L1[s], pS[:, 0:128], AF.Copy, [bpS], [bs["L1"]])
            ACTF(L2[s][:, 0:64], pS[:, 64:128], AF.Copy, [bpS], [bs["L2"]])
            ACTF(L2[s][:, 64:128], pS[:, 0:64], AF.Copy, [bpS], [bs["L2"]], scale=-1.0)

        def setup_dve_toe(g):
            s, s3, bs = slots(g)
            pS, bpS = psS[s]
            TT("dve", toet[s], pS[:, 128:256], maskv, ALU.mult, [bpS, B_sm["mask"], bs["L1"], bs["L2"]], [bs["toet"]])
            STT("dve", Toe[s3], identv, sm["Dv"][:, g:g + 1], toet[s], ALU.mult, ALU.add,
                [B_sm["ident"], B_sm["Dv"], bs["toet"]], [bs["Toe"]])

        def tabs(g):
            tsl = tab_slot[g]
            return EC[tsl][:, g % 2, :], ES[tsl][:, g % 2, :], B_EC[tsl], B_ES[tsl]

        def main_a_pe(g):
            s, s3, bs = slots(g)
            Ug = UG[:, g, :]
            MM(psS1[s][0][:, 0:NCH], L1[s], Ug, True, True, [bs["L1"], B_UG[g]], [psS1[s][1]], sig=True)
            MM(psS2[s][0][:, 0:NCH], L2[s], Ug, True, True, [bs["L2"], B_UG[g]], [psS2[s][1]], sig=True)

        def main_a_mod(g):
            s, s3, bs = slots(g)
            ec, es, bec, bes = tabs(g)
            TT("dve", M1[s], psS1[s][0][:, 0:NCH], ec, ALU.mult, [psS1[s][1], bec], [bs["M1"]])
            TT("dve", M2[s], psS2[s][0][:, 0:NCH], es, ALU.mult, [psS2[s][1], bes], [bs["M2"]])
            TT("pool", M1[s], M1[s], M2[s], ALU.add, [bs["M1"], bs["M2"]], [bs["M1"]])

        def main_a_scan(g):
            s, s3, bs = slots(g)
            ec, es, bec, bes = tabs(g)
            P.op("dve", (lambda s, g: lambda e: e.tensor_tensor_scan(
                out=M2[s], data0=t_r8[:, g:g + 1].to_broadcast([128, NCH]), data1=M1[s], initial=0.0,
                op0=ALU.mult, op1=ALU.add))(s, g), [bs["M1"], bt], [bs["M2"]])
            TT("dve", X1b[s][:, 1:NCH], M2[s][:, 0:NCH - 1], ec[:, 0:NCH - 1], ALU.mult, [bs["M2"], bec], [bs["X1"]])
            TT("dve", X2b[s][:, 1:NCH], M2[s][:, 0:NCH - 1], es[:, 0:NCH - 1], ALU.mult, [bs["M2"], bes], [bs["X2"]])

        def main_b(g):
            s, s3, bs = slots(g)
            Ug = UG[:, g, :]
            pY, bpY = psY[s]
            MM(pY[:, 0:NCH], Toe[s3], Ug, True, False, [bs["Toe"], B_UG[g]], [bpY])
            MM(pY[:, 0:NCH], R1[s3], X1b[s][:, 0:NCH], False, False, [bs["R1"], bs["X1"]], [bpY])
            MM(pY[:, 0:NCH], R2[s3], X2b[s][:, 0:NCH], False, True, [bs["R2"], bs["X2"]], [bpY], sig=True)
            ACTF(Ug, pY[:, 0:NCH], AF.Gelu, [bpY], [B_UG[g]])
            if g % 8 == 7:
                j = g // 8
                for tau in range(8):
                    dst = scrG[j, :, tau, :].rearrange("(g h) c -> h g c", h=16)
                    DMA("sp", dst, UG[16 * tau:16 * tau + 16, 8 * j:8 * j + 8, :], B_UG[8 * j:8 * j + 8], [B_scrG[j]], join=(tau > 0))
                DMA("sp", ugT[:, j, :, :], scrG[j], [B_scrG[j]], B_ug[8 * j:8 * j + 8])

        tables_act1(0)
        tables_dve(0)
        tables_act(0)
        for step in range(32 + 2):
            g0, g1, g2 = step, step - 1, step - 2
            v0, v1, v2 = g0 < 32, 0 <= g1 < 32, 0 <= g2 < 32
            gen1 = (step % 2 == 1) and (step + 1 < 32)
            gen2 = (step % 2 == 0) and (2 <= step < 32)
            if v2:
                main_b(g2)
            if v1:
                main_a_pe(g1)
            if v0:
                setup_dve_wq(g0)
                setup_pool_r(g0)
                setup_pe(g0)
            if v1:
                main_a_mod(g1)
            if v0:
                setup_act_l(g0)
            if gen1:
                tables_act1(step + 1)
            if gen2:
                tables_act(step)
            if v1:
                setup_dve_toe(g1)
                main_a_scan(g1)
            if gen1:
                tables_dve(step + 1)

        if dbg:
            for j in range(4):
                DMA("sp", dbg_d["gT"][:, j * 8 * NCH:(j + 1) * 8 * NCH], ugT[:, j, :, :].rearrange("p k c -> p (k c)"),
                    B_ug[8 * j:8 * j + 8], [])

        P.barrier()
        B_wup = [Buf(), Buf(), Buf()]
        DMA("pool", wup[:, 0, :, :], w_up_d[0], [], [B_wup[0]])
        DMA("pool", wup[:, 1, :, :], w_up_d[1], [], [B_wup[1]])

        ps_rot = Rot([(banks[i], bbuf[i]) for i in range(4)])
        ps_acc = Rot([(banks[4], bbuf[4]), (banks[5], bbuf[5])])
        ps_stat = (banks[6], bbuf[6])
        B_hn = [Buf(f"hn{i}") for i in range(5)]
        B_sqh, B_rstd2, B_sqrt2 = Buf(), Buf(), Buf()
        P.op("pool", lambda e: e.memset(hnT[:, :, 0:2], 0.0), [], [B_hn[0]])
        cob2 = [cobf, cobf_b]
        B_cob2 = [B_cobf, Buf()]
        sg2 = [sgb, tcb2[0]]
        B_sg2 = [B_sg, B_tc2[0]]

        def m2_z(ti, o):
            s0, V = TILES[ti]
            c0, c1 = s0 // 8, (s0 + V) // 8
            gview = lambda k: ugT[:, k, :, c0:c1].rearrange("p k c -> p c k")
            ps, bps = ps_rot.get()
            for k in range(4):
                MM(ps[:, 0:V], w_glu[:, k, 128 * o:128 * (o + 1)], gview(k), k == 0, k == 3, [B_wglu] + B_ug[8 * k:8 * k + 8], [bps], sig=(k == 3))
            sg, bsg = sg2[o % 2], B_sg2[o % 2]
            ACTF(sg[:, 0:V], ps[:, 0:V], AF.Sigmoid, [bps], [bsg])
            TT("dve", co[:, o, 0:V].rearrange("p (c k) -> p c k", k=8), gview(o), sg[:, 0:V].rearrange("p (c k) -> p c k", k=8),
               ALU.mult, B_ug[8 * o:8 * o + 8] + [bsg], [B_co])

        def m2_a_stats(ti):
            s0, V = TILES[ti]
            ACTF(sqb[:, 0:4, 0:V], co[:, :, 0:V], AF.Square, [B_co], [B_sqb])
            rms_stats(V, 4, lambda k: sqb[:, k, 0:V], B_sqb, 1.0 / 512, sqrt_t, sqrt_t, B_sqrt, B_sqrt)

        def m2_a_cobf(ti):
            s0, V = TILES[ti]
            cb, bcb = cob2[ti % 2], B_cob2[ti % 2]
            for j in range(4):
                STT("dve", cb[:, j, 0:V], co[:, j, 0:V], sm["gains"][:, j:j + 1], sqrt_t[:, 0:V],
                    ALU.mult, ALU.mult, [B_co, B_sm["gains"], B_sqrt], [bcb])

        def m2_wout(ti, o):
            s0, V = TILES[ti]
            bh = B_h[ti]
            cb, bcb = cob2[ti % 2], B_cob2[ti % 2]
            ps, bps = ps_acc.get()
            for k in range(4):
                MM(ps[:, 0:V], w_outh[:, k, 128 * o:128 * (o + 1)], cb[:, k, 0:V], k == 0, k == 3,
                   [B_wouth, bcb], [bps], sig=(k == 3))
            TT("dve", hT[:, o, s0 + 2:s0 + 2 + V], hT[:, o, s0 + 2:s0 + 2 + V], ps[:, 0:V], ALU.add, [bps, bh], [bh])

        def m2_b2sq(ti):
            s0, V = TILES[ti]
            bh = B_h[ti]
            ACTF(sqb[:, :, 0:V], hT[:, :, s0 + 2:s0 + 2 + V], AF.Square, [bh], [B_sqb])

        def m2_b2mm(ti):
            s0, V = TILES[ti]
            rms_stats(V, KD, lambda k: sqb[:, k, 0:V], B_sqb, 1.0 / D, rstd, rstd, B_rstd, B_rstd)

        def m2_b2s(ti):
            m2_b2sq(ti)
            m2_b2mm(ti)

        def m2_b2x(ti):
            s0, V = TILES[ti]
            bh = B_h[ti]
            for k in range(KD):
                STT("dve", hnT[:, k, s0 + 2:s0 + 2 + V], hT[:, k, s0 + 2:s0 + 2 + V], gffn[:, k:k + 1],
                    rstd[:, 0:V], ALU.mult, ALU.mult, [bh, B_sm["gffn"], B_rstd], [B_hn[ti]])

        for o in range(4):
            m2_z(0, o)
        m2_a_stats(0)
        m2_a_cobf(0)
        for ti in range(5):
            nxt = ti + 1 < 5
            for o in range(4):
                if nxt:
                    m2_z(ti + 1, o)
                m2_wout(ti, 2 * o)
                m2_wout(ti, 2 * o + 1)
                if o == 0 and ti > 0:
                    m2_b2sq(ti - 1)
                if o == 1 and ti > 0:
                    m2_b2mm(ti - 1)
                if o == 3 and ti > 0:
                    m2_b2x(ti - 1)
            if nxt:
                m2_a_stats(ti + 1)
                m2_a_cobf(ti + 1)
        m2_b2s(4)
        m2_b2x(4)

        if dbg:
            for k in range(KD):
                DMA("sp", dbg_d["h_mix"][:, k, :], hT[:, k, :], B_h, [])

        P.barrier()

        B_wdn = [Buf() for _ in range(10)]
        B_hid = [Buf() for _ in range(10)]
        B_a0, B_v0 = [Buf(), Buf(), Buf()], [Buf(), Buf(), Buf()]
        B_sa = [Buf(), Buf()]
        if OPT_FFN:
            psA = Rot([(banks[0], bbuf[0]), (banks[1], bbuf[1]), (banks[2], bbuf[2])])
            psB = Rot([(banks[3], bbuf[3]), (banks[4], bbuf[4]), (banks[5], bbuf[5])])
            psD = Rot([(banks[6], bbuf[6]), (banks[7], bbuf[7])])
        else:
            psA = Rot([(banks[0], bbuf[0]), (banks[1], bbuf[1])])
            psB = Rot([(banks[2], bbuf[2]), (banks[3], bbuf[3])])
            psD = Rot([(banks[4], bbuf[4]), (banks[5], bbuf[5])])
        fwa = sm["fwa"][:].rearrange("p (t f) -> p t f", t=3)
        fwv = sm["fwv"][:].rearrange("p (t f) -> p t f", t=3)
        fba, fbv = sm["fba"], sm["fbv"]
        rr = [0]

        def load_wup(f):
            DMA("pool", wup[:, f % 3, :, :], w_up_d[f], [], [B_wup[f % 3]])

        def load_wdn(f):
            DMA("pool", wdn[f % 10], w_dn_d[f], [], [B_wdn[f % 10]])

        def up_tile(f):
            ws = f % 3
            hs = f % 10
            for ti, (s0, V) in enumerate(TILES):
                N = V + 2
                bhn = [B_hn[ti]] + ([B_hn[ti - 1]] if ti > 0 else [])
                pa, bpa = psA.get()
                pb, bpb = psB.get()
                for k in range(KD):
                    MM(pa[:, 0:N], wup[:, ws, k, 0:128], hnT[:, k, s0:s0 + N], k == 0, k == KD - 1, [B_wup[ws]] + bhn, [bpa], sig=(k == KD - 1))
                for k in range(KD):
                    MM(pb[:, 0:N], wup[:, ws, k, 128:256], hnT[:, k, s0:s0 + N], k == 0, k == KD - 1, [B_wup[ws]] + bhn, [bpb], sig=(k == KD - 1))
                q = rr[0] % (3 if OPT_FFN else 2)
                rr[0] += 1
                a0, v0 = a0b[q], v0b[q]
                if OPT_FFN2:
                    ACTF(a0[:, 0:V], pa[:, 2:N], AF.Identity, [bpa, B_sm["fwa"], B_sm["fba"]], [B_a0[q]], bias=fba[:, f:f + 1], scale=fwa[:, 2, f:f + 1])
                    ACTF(v0[:, 0:V], pb[:, 2:N], AF.Identity, [bpb, B_sm["fwv"], B_sm["fbv"]], [B_v0[q]], bias=fbv[:, f:f + 1], scale=fwv[:, 2, f:f + 1])
                    STT("dve", a0[:, 0:V], pa[:, 1:N - 1], fwa[:, 1, f:f + 1], a0[:, 0:V], ALU.mult, ALU.add, [bpa, B_a0[q], B_sm["fwa"]], [B_a0[q]])
                    STT("dve", v0[:, 0:V], pb[:, 1:N - 1], fwv[:, 1, f:f + 1], v0[:, 0:V], ALU.mult, ALU.add, [bpb, B_v0[q], B_sm["fwv"]], [B_v0[q]])
                    STT("dve", a0[:, 0:V], pa[:, 0:N - 2], fwa[:, 0, f:f + 1], a0[:, 0:V], ALU.mult, ALU.add, [bpa, B_a0[q], B_sm["fwa"]], [B_a0[q]])
                    STT("dve", v0[:, 0:V], pb[:, 0:N - 2], fwv[:, 0, f:f + 1], v0[:, 0:V], ALU.mult, ALU.add, [bpb, B_v0[q], B_sm["fwv"]], [B_v0[q]])
                    ACTF(a0[:, 0:V], a0[:, 0:V], AF.Silu, [B_a0[q]], [B_a0[q]])
                    TT("pool", hid[hs][:, s0:s0 + V], a0[:, 0:V], v0[:, 0:V], ALU.mult, [B_a0[q], B_v0[q]], [B_hid[hs]])
                else:
                    sa = sab[q]
                    ACTF(a0[:, 0:V], pa[:, 2:N], AF.Identity, [bpa, B_sm["fwa"], B_sm["fba"]], [B_a0[q]], bias=fba[:, f:f + 1], scale=fwa[:, 2, f:f + 1])
                    STT("dve", a0[:, 0:V], pa[:, 1:N - 1], fwa[:, 1, f:f + 1], a0[:, 0:V], ALU.mult, ALU.add, [bpa, B_a0[q], B_sm["fwa"]], [B_a0[q]])
                    STT("dve", a0[:, 0:V], pa[:, 0:N - 2], fwa[:, 0, f:f + 1], a0[:, 0:V], ALU.mult, ALU.add, [bpa, B_a0[q], B_sm["fwa"]], [B_a0[q]])
                    ACTF(sa[:, 0:V], a0[:, 0:V], AF.Silu, [B_a0[q]], [B_sa[q]])
                    ACTF(v0[:, 0:V], pb[:, 2:N], AF.Identity, [bpb, B_sm["fwv"], B_sm["fbv"]], [B_v0[q]], bias=fbv[:, f:f + 1], scale=fwv[:, 2, f:f + 1])
                    STT("dve", v0[:, 0:V], pb[:, 1:N - 1], fwv[:, 1, f:f + 1], v0[:, 0:V], ALU.mult, ALU.add, [bpb, B_v0[q], B_sm["fwv"]], [B_v0[q]])
                    STT("dve", v0[:, 0:V], pb[:, 0:N - 2], fwv[:, 0, f:f + 1], v0[:, 0:V], ALU.mult, ALU.add, [bpb, B_v0[q], B_sm["fwv"]], [B_v0[q]])
                    TT("pool", hid[hs][:, s0:s0 + V], sa[:, 0:V], v0[:, 0:V], ALU.mult, [B_sa[q], B_v0[q]], [B_hid[hs]])

        def final_sq(ti):
            s0, V = TILES[ti]
            bh = B_h[ti]
            ACTF(sqh[:, :, 0:V], hT[:, :, s0 + 2:s0 + 2 + V], AF.Square, [bh], [B_sqh])

        def final_mm(ti):
            s0, V = TILES[ti]
            rms_stats(V, KD, lambda k: sqh[:, k, 0:V], B_sqh, 1.0 / D, rstd2, rstd2, B_rstd2, B_rstd2, pss=(banks[ti % 3], bbuf[ti % 3]))

        def final_stats(ti):
            final_sq(ti)
            final_mm(ti)

        def final_scale(ti, k):
            s0, V = TILES[ti]
            bh = B_h[ti]
            STT("dve", hT[:, k, s0 + 2:s0 + 2 + V], hT[:, k, s0 + 2:s0 + 2 + V], gfin[:, k:k + 1],
                rstd2[:, 0:V], ALU.mult, ALU.mult, [bh, B_sm["gfin"], B_rstd2], [bh])

        def final_store(ti):
            s0, V = TILES[ti]
            t0 = max(s0, NMETA)
            outs.append(DMA("sp", yT_d[:, :, t0 - NMETA:s0 + V - NMETA], hT[:, :, t0 + 2:s0 + 2 + V], [B_h[ti]], []))

        def down_group(fs, last=False):
            for ti, (s0, V) in enumerate(TILES):
                nk = 0
                for o in range(KD):
                    ps, bps = psD.get()
                    for i, f in enumerate(fs):
                        MM(ps[:, 0:V], wdn[f % 10][:, 128 * o:128 * (o + 1)], hid[f % 10][:, s0:s0 + V], i == 0, i == len(fs) - 1,
                           [B_wdn[f % 10], B_hid[f % 10]], [bps], sig=(i == len(fs) - 1))
                    TT("dve", hT[:, o, s0 + 2:s0 + 2 + V], hT[:, o, s0 + 2:s0 + 2 + V], ps[:, 0:V], ALU.add, [bps, B_h[ti]], [B_h[ti]])
                    if last and ti >= 1:
                        if o == 0:
                            final_sq(ti - 1)
                        if o == 2:
                            final_mm(ti - 1)
                        if o >= 4:
                            final_scale(ti - 1, nk)
                            nk += 1
                if last and ti >= 1:
                    while nk < KD:
                        final_scale(ti - 1, nk)
                        nk += 1
                    final_store(ti - 1)
            if last:
                ti = len(TILES) - 1
                final_stats(ti)
                for k in range(KD):
                    final_scale(ti, k)
                final_store(ti)

        outs = []
        load_wdn(0)
        for gi, fs in enumerate(FGROUPS):
            for f in fs:
                if f + 2 < NF:
                    load_wup(f + 2)
                if f + 1 < NF:
                    load_wdn(f + 1)
                up_tile(f)
                if gi > 0 and f == fs[0]:
                    down_group(FGROUPS[gi - 1])
        down_group(FGROUPS[-1], last=True)

        fin = P.op("sp", lambda e: None, sig=False)
        fin.deps.extend(P.all_dmas)
        fin.deps.extend(outs)
        P.emit(block)
    return nc


def _kmajor(w):
    K = w.shape[0] // 128
    return np.ascontiguousarray(w.reshape(K, 128, w.shape[1]).transpose(1, 0, 2))


def _percol(v, K):
    return np.ascontiguousarray(v.reshape(K, 128).T)


def _prep_shared(inp):
    f = lambda a: np.asarray(a, dtype=np.float32)
    sh = {}
    sh["w_in"] = _kmajor(f(inp["w_in"])[0])
    sh["w_out"] = _kmajor(f(inp["w_out"])[0])
    sh["w_glu"] = _kmajor(f(inp["ssm_w_glu"])[0])
    wu = _kmajor(f(inp["w_up"])[0])
    wa = wu[:, :, :DFF].reshape(128, KD, NF, 128)
    wv = wu[:, :, DFF:].reshape(128, KD, NF, 128)
    sh["w_up"] = np.ascontiguousarray(np.concatenate([wa, wv], axis=3).transpose(2, 0, 1, 3))
    sh["w_dn"] = np.ascontiguousarray(f(inp["w_down"])[0].reshape(NF, 128, 1024))
    sh["gmix"] = _percol(f(inp["norm_mix_g"])[0], 8)
    sh["gffn"] = _percol(f(inp["norm_ffn_g"])[0], 8)
    sh["gfin"] = _percol(f(inp["norm_final_g"]), 8)
    sh["gainc"] = _percol(f(inp["gain_conv_out"])[0], 4)
    sh["gains"] = _percol(f(inp["gain_ssm_out"])[0], 4)
    cw = f(inp["conv_w"])[0]
    sh["cw"] = np.ascontiguousarray(cw.reshape(3, 4, 128).transpose(2, 0, 1).reshape(128, 12))
    fw = f(inp["ffn_conv_w"])[0]
    sh["fwa"] = np.ascontiguousarray(fw[:, :DFF].reshape(3, NF, 128).transpose(2, 0, 1).reshape(128, 66))
    sh["fwv"] = np.ascontiguousarray(fw[:, DFF:].reshape(3, NF, 128).transpose(2, 0, 1).reshape(128, 66))
    fb = f(inp["ffn_conv_b"])[0]
    sh["fba"] = np.ascontiguousarray(fb[:DFF].reshape(NF, 128).T)
    sh["fbv"] = np.ascontiguousarray(fb[DFF:].reshape(NF, 128).T)
    dup = lambda a: np.ascontiguousarray(np.concatenate([a, a], axis=0))
    stack = lambda a, b: np.ascontiguousarray(np.concatenate([a, b], axis=0))
    sh["lr"] = dup(f(inp["ssm_lam_re"])[0].T)
    sh["li"] = dup(f(inp["ssm_lam_im"])[0].T)
    sh["ldt"] = np.ascontiguousarray(np.broadcast_to(f(inp["ssm_log_dt"])[0][None, :], (128, 32)))
    bre = f(inp["ssm_b_re"])[0].transpose(1, 0, 2).reshape(64, 512)
    bim = f(inp["ssm_b_im"])[0].transpose(1, 0, 2).reshape(64, 512)
    sh["X1"] = stack(bre, bim)
    sh["X2"] = stack(bim, bre)
    cre = f(inp["ssm_c_re"])[0].transpose(2, 0, 1).reshape(64, 512)
    cim = f(inp["ssm_c_im"])[0].transpose(2, 0, 1).reshape(64, 512)
    sh["CX1"] = stack(cre, cim)
    sh["CX2"] = stack(cim, cre)
    dd = f(inp["ssm_d"])[0]
    sh["Dv"] = np.ascontiguousarray(np.tile(dd.T, (8, 1)))
    sh["ident"] = np.eye(128, dtype=np.float32)
    kk = np.arange(128) // 16
    sh["mask"] = (kk[None, :] >= kk[:, None]).astype(np.float32)
    sh["cidx"] = np.ascontiguousarray(np.broadcast_to(np.arange(NCH, dtype=np.float32)[None, :], (128, NCH)))
    sh["jv"] = np.ascontiguousarray(np.broadcast_to(np.array(JV, dtype=np.float32)[None, :], (128, 24)))
    sh["sgn"] = np.where(np.arange(128) < 64, -1.0, 1.0).astype(np.float32)[:, None]
    return sh


def _prep_x(x_b, meta):
    h0 = np.concatenate([meta, x_b], axis=0)
    hT = h0.T.reshape(KD, 128, L).transpose(1, 0, 2)
    out = np.zeros((128, KD, LP), dtype=np.float32)
    out[:, :, 2:] = hT
    return out


_NC_CACHE = {}


def kernel(**inputs):
    x = np.asarray(inputs["x"], dtype=np.float32)
    meta = np.asarray(inputs["meta_tokens"], dtype=np.float32)
    sh = _prep_shared(inputs)
    dbg = bool(inputs.get("_dbg", False)) if isinstance(inputs, dict) else False
    if dbg not in _NC_CACHE:
        _NC_CACHE[dbg] = _build(dbg)
    nc = _NC_CACHE[dbg]
    B = x.shape[0]
    in_maps = []
    for b in range(B):
        m = dict(sh)
        m["xT"] = _prep_x(x[b], meta)
        in_maps.append(m)
    res = run_bass_kernel_spmd(nc, in_maps, core_ids=list(range(B)))
    out = np.empty((B, 2048, D), dtype=np.float32)
    for b in range(B):
        yT = np.asarray(res.results[b]["yT"])
        out[b] = yT.transpose(2, 1, 0).reshape(2048, D)
    if dbg:
        kernel.last = res
    return out
```

```python
import math
OPT_SSM = True
OPT_PASS = True
OPT_FFN = True
OPT_FFN2 = True
OPT_TAB = True
OPT_MAT = True
OPT_MATQ = True
OPT_MATR = True
from contextlib import ExitStack

import numpy as np
import concourse.bass as bass
import concourse.mybir as mybir
from concourse.bass_utils import run_bass_kernel_spmd

F32 = mybir.dt.float32
BF16 = mybir.dt.bfloat16
ALU = mybir.AluOpType
AF = mybir.ActivationFunctionType

D = 1024
L = 2064
NMETA = 16
LP = L + 2
KD = 8
DFF = 2816
NF = 22
NCH = 258
EPS = 1e-6
TILES = [(416 * i, min(416, L - 416 * i)) for i in range(5)]
TWO_PI_S = 2.0 * math.pi * (1.0 - 2e-6)
MAGIC = 12582912.0
JV = [7, 6, 5, 4, 3, 2, 1, 0, 1, 2, 3, 4, 5, 6, 7, 8, -7, -6, -5, -4, -3, -2, -1, 0]
FGROUPS = [list(range(0, 8)), list(range(8, 15)), list(range(15, 22))]


class Buf:
    __slots__ = ("name", "lws", "rd")

    def __init__(self, name=""):
        self.name = name
        self.lws = []
        self.rd = []


class Op:
    __slots__ = ("eng", "fn", "deps", "sig", "idx", "ev", "dma")

    def __init__(self, eng, fn, sig, dma):
        self.eng = eng
        self.fn = fn
        self.deps = []
        self.sig = sig
        self.idx = None
        self.ev = None
        self.dma = dma


class Prog:
    ENGS = ("pe", "act", "dve", "pool", "sp")

    def __init__(self, nc, ndma_sems=24):
        self.nc = nc
        self.ops = {e: [] for e in self.ENGS}
        self.sems = {}
        self.ndma = ndma_sems
        self.dma_ring = {}
        self.dma_cnt = {}
        self.dma_last = {}
        self.dma_i = {}
        self.bar = []
        self.all_dmas = []

    def open(self, stack):
        nc = self.nc
        for e in self.ENGS:
            self.sems[e] = stack.enter_context(nc.semaphore("s_" + e))
        for q in ("sp", "act", "pool"):
            n = self.ndma if q == "sp" else 8
            self.dma_ring[q] = [stack.enter_context(nc.semaphore(f"d_{q}{i}")) for i in range(n)]
            self.dma_cnt[q] = [0] * n
            self.dma_last[q] = [None] * n
            self.dma_i[q] = 0

    def op(self, eng, fn, reads=(), writes=(), sig=True, join=False):
        o = Op(eng, fn, sig, None)
        o.deps.extend(self.bar)
        self._deps(o, reads, writes, join)
        o.idx = len(self.ops[eng])
        self.ops[eng].append(o)
        return o

    def dma(self, q, fn, reads=(), writes=(), join=False):
        n = len(self.dma_ring[q])
        i = self.dma_i[q]
        self.dma_i[q] = (i + 1) % n
        o = Op(q, fn, False, (q, i))
        o.deps.extend(self.bar)
        prev = self.dma_last[q][i]
        if prev is not None:
            o.deps.append(prev)
        self._deps(o, reads, writes, join)
        self.dma_cnt[q][i] += 16
        o.ev = (self.dma_ring[q][i], self.dma_cnt[q][i])
        self.dma_last[q][i] = o
        o.idx = len(self.ops[q])
        self.ops[q].append(o)
        self.all_dmas.append(o)
        return o

    def _deps(self, o, reads, writes, join=False):
        for r in reads:
            o.deps.extend(r.lws)
        for w in writes:
            if join and w.lws and not w.rd:
                continue
            o.deps.extend(w.lws)
            o.deps.extend(w.rd)
        for r in reads:
            r.rd.append(o)
        for w in writes:
            if join and w.lws and not w.rd:
                w.lws.append(o)
            else:
                w.lws = [o]
                w.rd = []

    def barrier(self):
        deps = []
        for e in self.ENGS:
            comp = [o for o in self.ops[e] if o.dma is None]
            if comp:
                comp[-1].sig = True
                deps.append(comp[-1])
        deps.extend(self.all_dmas)
        self.all_dmas = []
        self.bar = deps

    def finalize(self):
        for e in self.ENGS:
            lst = [o for o in self.ops[e] if o.dma is None]
            if lst:
                lst[-1].sig = True
            cnt = 0
            pending = []
            for o in lst:
                pending.append(o)
                if o.sig:
                    cnt += 1
                    for p in pending:
                        p.ev = (self.sems[e], cnt)
                    pending = []

    def emit(self, block):
        self.finalize()

        def run(e):
            def body(eng):
                waited = {}
                for o in self.ops[e]:
                    need = {}
                    for d in o.deps:
                        if d.dma is None and d.eng == e:
                            if e == "pe" and o.dma is None:
                                continue
                            if not d.sig:
                                nxt = [x for x in self.ops[e][d.idx:] if x.dma is None and x.sig][0]
                                if nxt.idx >= o.idx:
                                    continue
                        s, v = d.ev
                        if need.get(s.name, (None, 0))[1] < v:
                            need[s.name] = (s, v)
                    for sn, (s, v) in need.items():
                        if waited.get(sn, 0) < v:
                            eng.wait_ge(s, v)
                            waited[sn] = v
                    ins = o.fn(eng)
                    if ins is None:
                        continue
                    if o.dma is not None:
                        ins.then_inc(o.ev[0], 16)
                    elif o.sig:
                        ins.then_inc(o.ev[0], 1)
            return body

        block.tensor(run("pe"))
        block.scalar(run("act"))
        block.vector(run("dve"))
        block.gpsimd(run("pool"))
        block.sync(run("sp"))


class Rot:
    def __init__(self, items):
        self.items = items
        self.i = 0

    def get(self):
        it = self.items[self.i]
        self.i = (self.i + 1) % len(self.items)
        return it


def _build(dbg=False):
    nc = bass.Bass("TRN2", target_bir_lowering=False)

    def din(name, shape):
        return nc.dram_tensor(name, list(shape), F32, kind="ExternalInput").ap()

    xT_d = din("xT", [128, KD, LP])
    w_in_d = din("w_in", [128, KD, 2048])
    w_out_d = din("w_out", [128, KD, 1024])
    w_glu_d = din("w_glu", [128, 4, 512])
    w_up_d = din("w_up", [NF, 128, KD, 256])
    w_dn_d = din("w_dn", [NF, 128, 1024])
    small_specs = dict(gmix=8, gffn=8, gfin=8, gainc=4, gains=4, cw=12, fwa=66, fwv=66, fba=22, fbv=22,
                       lr=32, li=32, ldt=32, X1=512, X2=512, CX1=512, CX2=512, Dv=32,
                       ident=128, mask=128, cidx=NCH, jv=24, sgn=1)
    small_d = {k: din(k, [128, n]) for k, n in small_specs.items()}
    yT_d = nc.dram_tensor("yT", [128, KD, 2048], F32, kind="ExternalOutput").ap()
    scrU = nc.dram_tensor("scrU", [4, 128, 8, NCH], BF16).ap()
    scrG = nc.dram_tensor("scrG", [4, 128, 8, NCH], BF16).ap()
    dbg_d = {}
    if dbg:
        dbg_d["h_mix"] = nc.dram_tensor("h_mix", [128, KD, LP], F32, kind="ExternalOutput").ap()
        dbg_d["gT"] = nc.dram_tensor("gT", [128, 4 * 8 * NCH], BF16, kind="ExternalOutput").ap()
        dbg_d["uT"] = nc.dram_tensor("uT", [128, 4 * 8 * NCH], BF16, kind="ExternalOutput").ap()
        dbg_d["pr"] = nc.dram_tensor("pr", [128, 32 * 24], F32, kind="ExternalOutput").ap()
        dbg_d["pi"] = nc.dram_tensor("pi", [128, 32 * 24], F32, kind="ExternalOutput").ap()

    with ExitStack() as st:
        def sbt(name, shape, dt):
            return st.enter_context(nc.sbuf_tensor(name, list(shape), dt))

        hT = sbt("hT", [128, KD, LP], F32)
        sm = {k: sbt("sm_" + k, [128, n], F32) for k, n in small_specs.items()}
        ones_bf = sbt("ones_bf", [128, 128], BF16)
        tiny = sbt("tiny", [128, 16, 32], F32)

        ARENA_BYTES = 128 * 1024
        arena = sbt("arena", [128, ARENA_BYTES // 2], BF16)

        class Carver:
            def __init__(self):
                self.off = 0

            def take(self, shape, dt):
                n = int(np.prod(shape))
                nb = n * (4 if dt == F32 else 2)
                nb = (nb + 63) // 64 * 64
                assert self.off + nb <= ARENA_BYTES, (self.off, nb)
                v = arena[:, self.off // 2:(self.off + nb) // 2]
                if dt == F32:
                    v = v.bitcast(F32)
                v = v[:, 0:n]
                if len(shape) == 2:
                    v = v.rearrange("p (a b) -> p a b", a=shape[0], b=shape[1])
                elif len(shape) == 3:
                    v = v.rearrange("p (a b c) -> p a b c", a=shape[0], b=shape[1], c=shape[2])
                self.off += nb
                return v

        cm = Carver()
        w_in = cm.take([KD, 2048], BF16)
        xbf = cm.take([KD, 418], BF16)
        sqb = cm.take([KD, 418], BF16)
        regionA_end = cm.off
        w_outh = cm.take([4, 1024], BF16)
        w_glu = cm.take([4, 512], BF16)
        ugT = cm.take([4, 8, NCH], BF16)
        rstd = cm.take([418], F32)
        sqrt_t = cm.take([418], F32)
        tcb2 = [cm.take([418], F32) for _ in range(2)]
        cvb2 = [cm.take([418], F32) for _ in range(2)]
        tb2 = [cm.take([416], F32) for _ in range(2)]
        co = cm.take([4, 416], F32)
        cobf = cm.take([4, 416], BF16)
        sgb = cm.take([416], F32)
        cobf_b = cm.take([4, 416], BF16)
        cs_ = Carver()
        UG = cs_.take([32, NCH], BF16)
        Wf = [cs_.take([128], F32) for _ in range(2)]
        Qf = [cs_.take([128], F32) for _ in range(2)]
        sT2 = [cs_.take([128], F32) for _ in range(2)]
        L1 = [cs_.take([128], BF16) for _ in range(2)]
        L2 = [cs_.take([128], BF16) for _ in range(2)]
        Toe = [cs_.take([128], BF16) for _ in range(3)]
        R1 = [cs_.take([128], BF16) for _ in range(3)]
        R2 = [cs_.take([128], BF16) for _ in range(3)]
        toet = [cs_.take([128], F32) for _ in range(2)]
        pA = [cs_.take([128], F32) for _ in range(2)]
        pB = [cs_.take([128], F32) for _ in range(2)]
        ES = [cs_.take([2, NCH], F32) for _ in range(2)]
        EC = [cs_.take([2, NCH], F32) for _ in range(2)]
        PH = cs_.take([2, NCH], F32)
        PT = cs_.take([2, NCH], F32)
        M1 = [cs_.take([NCH], F32) for _ in range(2)]
        M2 = [cs_.take([NCH], F32) for _ in range(2)]
        X1b = [cs_.take([NCH + 2], BF16) for _ in range(2)]
        X2b = [cs_.take([NCH + 2], BF16) for _ in range(2)]
        assert cs_.off <= regionA_end, (cs_.off, regionA_end)
        ssm_param_start = cm.off
        T1h = [cm.take([16, 24], F32) for _ in range(2)]
        T2h = [cm.take([16, 24], F32) for _ in range(2)]
        TB1 = cm.take([32, 16], F32)
        TB2 = cm.take([32, 16], F32)
        PRa = cm.take([32, 24], F32)
        PIa = cm.take([32, 24], F32)
        BA = cm.take([32, 16], F32)
        SBB = cm.take([32, 16], F32)
        SCX1 = cm.take([32, 16], F32)
        NSCX1 = cm.take([32, 16], F32)
        NCX2 = cm.take([32, 16], F32)
        mixer_end = cm.off
        cf = Carver()
        hnT = cf.take([KD, LP], BF16)
        assert cf.off <= regionA_end + 8 * 1024 or True
        hid = [cf.take([L], BF16) for _ in range(10)]
        wdn = [cf.take([1024], BF16) for _ in range(10)]
        a0b = [cf.take([416], F32) for _ in range(3)]
        v0b = [cf.take([416], F32) for _ in range(3)]
        sab = [a0b[2], v0b[2]]
        sqh = cf.take([KD, 416], BF16)
        rstd2 = cf.take([416], F32)
        sqrt2 = cf.take([416], F32)
        assert cf.off >= ssm_param_start, (cf.off, ssm_param_start)
        wup = cf.take([3, KD, 256], BF16)
        assert 2 * KD * LP <= 32768 + 2 * KD * 418

        banks = [st.enter_context(nc.psum_tensor(f"ps{i}", [128, 512], F32)) for i in range(8)]
        bbuf = [Buf(f"ps{i}") for i in range(8)]

        P = Prog(nc)
        P.open(st)
        block = st.enter_context(nc.Block())

        def TT(eng, out, in0, in1, op, r, w, sig=True):
            return P.op(eng, lambda e: e.tensor_tensor(out=out, in0=in0, in1=in1, op=op), r, w, sig)

        def TS(eng, out, in0, s1, s2, op0, op1, r, w):
            if s2 is None:
                return P.op(eng, lambda e: e.tensor_scalar(out=out, in0=in0, scalar1=s1, scalar2=None, op0=op0), r, w)
            return P.op(eng, lambda e: e.tensor_scalar(out=out, in0=in0, scalar1=s1, scalar2=s2, op0=op0, op1=op1), r, w)

        def STT(eng, out, in0, sc, in1, op0, op1, r, w):
            return P.op(eng, lambda e: e.scalar_tensor_tensor(out=out, in0=in0, scalar=sc, in1=in1, op0=op0, op1=op1), r, w)

        def ACTF(out, in_, func, r, w, bias=0.0, scale=1.0):
            return P.op("act", lambda e: e.activation(out=out, in_=in_, func=func, bias=bias, scale=scale), r, w)

        def MM(out, lhsT, rhs, start, stop, r, w, sig=False):
            return P.op("pe", lambda e: e.matmul(out, lhsT=lhsT, rhs=rhs, start=start, stop=stop), r, w, sig)

        def DMA(q, out, in_, r, w, join=False):
            return P.dma(q, lambda e: e.dma_start(out=out, in_=in_), r, w, join)

        def round_sub(eng, x, tmp, bx, bt):
            TS(eng, tmp, x, MAGIC, None, ALU.add, None, [bx], [bt])
            TS(eng, tmp, tmp, -MAGIC, None, ALU.add, None, [bt], [bt])
            TT(eng, x, x, tmp, ALU.subtract, [bx, bt], [bx])

        B_h = [Buf(f"h{i}") for i in range(5)]
        B_sm = {k: Buf("sm_" + k) for k in small_specs}
        B_ones = Buf("ones")
        B_win = Buf("w_in")
        B_winj = [Buf(f"w_in{j}") for j in range(4)]
        B_wouth = Buf("w_outh")
        B_wglu = Buf("w_glu")
        B_xbf = Buf("xbf")
        B_sqb = Buf("sqb")
        B_rstd = Buf("rstd")
        B_sqrt = Buf("sqrt")
        B_ug = [Buf(f"ugT{g}") for g in range(32)]
        B_UG = [Buf(f"UG{g}") for g in range(32)]
        B_tc2, B_cv2, B_tb2 = [Buf(), Buf()], [Buf(), Buf()], [Buf(), Buf()]
        B_co, B_cobf, B_sg = Buf(), Buf(), Buf()
        B_PR, B_PI, B_BA, B_SBB, B_SCX1, B_tiny = Buf(), Buf(), Buf(), Buf(), Buf(), Buf()
        B_T1h, B_T2h, B_TB1, B_TB2 = [Buf(), Buf()], [Buf(), Buf()], Buf(), Buf()

        for k, ap in small_d.items():
            DMA("sp", sm[k][:], ap, [], [B_sm[k]])
        w_in4 = w_in.rearrange("p k (b c) -> p k b c", b=4)
        w_in_d4 = w_in_d.rearrange("p k (b c) -> p k b c", b=4)
        for j in range(4):
            DMA("pool", w_in4[:, :, :, 128 * j:128 * (j + 1)], w_in_d4[:, :, :, 128 * j:128 * (j + 1)], [], [B_winj[j]])
        for ti, (s0, V) in enumerate(TILES):
            lo = 0 if ti == 0 else s0 + 2
            DMA("sp", hT[:, :, lo:s0 + 2 + V], xT_d[:, :, lo:s0 + 2 + V], [B_win] if ti >= 1 else [], [B_h[ti]])
        P.op("dve", lambda e: e.memset(ones_bf[:], 1.0), [], [B_ones])

        tn = lambda i: tiny[:, i, :]
        lr, li, ldt = sm["lr"][:], sm["li"][:], sm["ldt"][:]
        sgn = sm["sgn"][:, 0:1]
        t_dt, t_y8, t_th, t_r8, t_f8, t_nr, t_den, t_fr, t_fi, t_a, t_b = [tn(i) for i in range(11)]
        bt = B_tiny
        X1v = sm["X1"][:].rearrange("p (g h) -> p g h", g=32)
        X2v = sm["X2"][:].rearrange("p (g h) -> p g h", g=32)
        CX1v = sm["CX1"][:].rearrange("p (g h) -> p g h", g=32)
        CX2v = sm["CX2"][:].rearrange("p (g h) -> p g h", g=32)

        B_den = Buf()

        def ssm_params_part0():
            TT("pool", t_den, lr, lr, ALU.mult, [B_sm["lr"]], [B_den])
            TT("pool", t_b, li, li, ALU.mult, [B_sm["li"]], [B_den])
            TT("pool", t_den, t_den, t_b, ALU.add, [B_den], [B_den])
            ACTF(t_dt, ldt, AF.Exp, [B_sm["ldt"]], [bt])
            TT("pool", t_y8, lr, t_dt, ALU.mult, [B_sm["lr"], bt], [bt])
            TT("pool", t_th, li, t_dt, ALU.mult, [B_sm["li"], bt], [bt])
            TS("pool", t_th, t_th, 1.0 / (2.0 * math.pi), None, ALU.mult, None, [bt], [bt])
            jv3 = sm["jv"][:].unsqueeze(1).to_broadcast([128, 16, 24])
            for half in range(2):
                gs = slice(16 * half, 16 * half + 16)
                T1, T2, B_T1, B_T2 = T1h[half], T2h[half], B_T1h[half], B_T2h[half]
                y8b = t_y8[:, gs].unsqueeze(2).to_broadcast([128, 16, 24])
                thb = t_th[:, gs].unsqueeze(2).to_broadcast([128, 16, 24])
                TT("pool", T1, jv3, y8b, ALU.mult, [B_sm["jv"], bt], [B_T1])
                TS("pool", T2, T1, 1.0 / 6.0, 1.0, ALU.mult, ALU.add, [B_T1], [B_T2])
                for kk in (5, 4, 3, 2, 1):
                    TT("pool", T2, T2, T1, ALU.mult, [B_T2, B_T1], [B_T2])
                    TS("pool", T2, T2, 1.0 / kk, 1.0, ALU.mult, ALU.add, [B_T2], [B_T2])
                TT("pool", T1, jv3, thb, ALU.mult, [B_sm["jv"], bt], [B_T1])
                round_sub("pool", T1, PRa[:, gs, :], B_T1, B_PR)
                P.op("pool", (lambda gs, T2: lambda e: e.tensor_copy(out=t_r8[:, gs], in_=T2[:, :, 15]))(gs, T2), [B_T2], [bt])
            TS("pool", t_f8, t_th, 8.0, None, ALU.mult, None, [bt], [bt])
            round_sub("pool", t_f8, t_a, bt, bt)
            TS("pool", SCX1, CX1v, sgn, None, ALU.mult, None, [B_sm["CX1"], B_sm["sgn"]], [B_SCX1])
            TS("pool", NSCX1, SCX1, -1.0, None, ALU.mult, None, [B_SCX1], [B_SCX1])
            TS("pool", NCX2, CX2v, -1.0, None, ALU.mult, None, [B_sm["CX2"]], [B_SCX1])

        def ssm_params_part1():
            P.op("dve", lambda e: e.reciprocal(out=t_den, in_=t_den), [B_den], [B_den])
            for half in range(2):
                gs = slice(16 * half, 16 * half + 16)
                T1, T2, B_T1, B_T2 = T1h[half], T2h[half], B_T1h[half], B_T2h[half]
                ACTF(PIa[:, gs, :], T1, AF.Sin, [B_T1], [B_PI], scale=TWO_PI_S)
                ACTF(T1, T1, AF.Abs, [B_T1], [B_T1])
                ACTF(PRa[:, gs, :], T1, AF.Sin, [B_T1], [B_PR], scale=-TWO_PI_S, bias=TWO_PI_S / 4.0)
                TT("pool", PIa[:, gs, :], PIa[:, gs, :], T2, ALU.mult, [B_PI, B_T2], [B_PI])
                TT("pool", PRa[:, gs, :], PRa[:, gs, :], T2, ALU.mult, [B_PR, B_T2], [B_PR])
            ar = PRa[:, :, 6]
            ai = PIa[:, :, 6]
            TS("pool", t_nr, ar, -1.0, None, ALU.add, None, [B_PR], [bt])
            TT("pool", t_fr, t_nr, lr, ALU.mult, [bt, B_sm["lr"]], [bt])
            TT("pool", t_a, ai, li, ALU.mult, [B_PI, B_sm["li"]], [bt])
            TT("pool", t_fr, t_fr, t_a, ALU.add, [bt], [bt])
            TT("pool", t_fr, t_fr, t_den, ALU.mult, [bt, B_den], [bt])
            TT("pool", t_fi, ai, lr, ALU.mult, [B_PI, B_sm["lr"]], [bt])
            TT("pool", t_a, t_nr, li, ALU.mult, [bt, B_sm["li"]], [bt])
            TT("pool", t_fi, t_fi, t_a, ALU.subtract, [bt], [bt])
            TT("pool", t_fi, t_fi, t_den, ALU.mult, [bt, B_den], [bt])
            TS("pool", t_fi, t_fi, sgn, None, ALU.mult, None, [bt, B_sm["sgn"]], [bt])
            frb = t_fr.unsqueeze(2).to_broadcast([128, 32, 16])
            fib = t_fi.unsqueeze(2).to_broadcast([128, 32, 16])
            TT("pool", TB1, frb, X1v, ALU.mult, [bt, B_sm["X1"]], [B_TB1])
            TT("pool", TB2, fib, X2v, ALU.mult, [bt, B_sm["X2"]], [B_TB2])
            TT("pool", BA, TB1, TB2, ALU.add, [B_TB1, B_TB2], [B_BA])
            TT("pool", TB1, frb, X2v, ALU.mult, [bt, B_sm["X2"]], [B_TB1])
            TT("pool", TB2, fib, X1v, ALU.mult, [bt, B_sm["X1"]], [B_TB2])
            TT("pool", SBB, TB1, TB2, ALU.subtract, [B_TB1, B_TB2], [B_SBB])
            TS("pool", SBB, SBB, sgn, None, ALU.mult, None, [B_SBB, B_sm["sgn"]], [B_SBB])

        ssm_params_part0()
        DMA("pool", w_outh, w_out_d[:, 0:4, :], [], [B_wouth])
        DMA("pool", w_glu, w_glu_d, [], [B_wglu])

        ps_rot = Rot([(banks[i], bbuf[i]) for i in range(6)])
        ps_stat = (banks[6], bbuf[6])
        ps_acc = Rot([(banks[7], bbuf[7]), (banks[6], bbuf[6])])
        gmix, gffn, gfin = sm["gmix"], sm["gffn"], sm["gfin"]
        cwv = sm["cw"][:].rearrange("p (t j) -> p t j", t=3)

        def rms_stats(n, nk, src_sq, bsq, scale, out_rstd, out_sqrt, brstd, bsqrt, pss=None):
            ps, bps = pss if pss is not None else ps_stat
            for k in range(nk):
                MM(ps[:, 0:n], ones_bf[:], src_sq(k), k == 0, k == nk - 1, [B_ones, bsq], [bps], sig=(k == nk - 1))
            ACTF(out_sqrt[:, 0:n], ps[:, 0:n], AF.Ln, [bps], [bsqrt], bias=EPS, scale=scale)
            ACTF(out_rstd[:, 0:n], out_sqrt[:, 0:n], AF.Exp, [bsqrt], [brstd], scale=-0.5)

        def pass_prologue_sq(ti):
            s0, V = TILES[ti]
            bh = B_h[ti]
            ACTF(sqb[:, :, 0:V], hT[:, :, s0 + 2:s0 + 2 + V], AF.Square, [bh], [B_sqb])

        def pass_prologue_mm(ti):
            s0, V = TILES[ti]
            rms_stats(V, KD, lambda k: sqb[:, k, 0:V], B_sqb, 1.0 / D, rstd, rstd, B_rstd, B_rstd)

        def pass_prologue_stats(ti):
            pass_prologue_sq(ti)
            pass_prologue_mm(ti)

        def pass_prologue_xbf(ti):
            s0, V = TILES[ti]
            N = V + 2
            bh = B_h[ti]
            if ti == 0:
                P.op("dve", lambda e: e.memset(xbf[:, :, 0:2], 0.0), [], [B_xbf])
            else:
                Np = TILES[ti - 1][1] + 2
                P.op("dve", (lambda Np: lambda e: e.tensor_copy(out=xbf[:, :, 0:2], in_=xbf[:, :, Np - 2:Np]))(Np), [B_xbf], [B_xbf])
            for k in range(KD):
                STT("dve", xbf[:, k, 2:N], hT[:, k, s0 + 2:s0 + 2 + V], gmix[:, k:k + 1], rstd[:, 0:V],
                    ALU.mult, ALU.mult, [bh, B_sm["gmix"], B_rstd], [B_xbf])

        def pass_unit(ti, j):
            s0, V = TILES[ti]
            N = V + 2
            c0, c1 = s0 // 8, (s0 + V) // 8
            if True:
                ps, bps = ps_rot.get()
                for k in range(KD):
                    MM(ps[:, 0:N], w_in[:, k, 1536 + 128 * j:1536 + 128 * (j + 1)], xbf[:, k, 0:N], k == 0, k == KD - 1,
                       [B_winj[j], B_xbf], [bps], sig=(k == KD - 1))
                pu, bpu = ps, bps
                psc, bpsc = ps_rot.get()
                psv, bpsv = ps_rot.get()
                psb, bpsb = ps_rot.get()
                for (ps, bps, col) in ((psc, bpsc, 512 + 128 * j), (psv, bpsv, 1024 + 128 * j), (psb, bpsb, 128 * j)):
                    for k in range(KD):
                        MM(ps[:, 0:N], w_in[:, k, col:col + 128], xbf[:, k, 0:N], k == 0, k == KD - 1,
                           [B_winj[j], B_xbf], [bps], sig=(k == KD - 1))
                jj = j % 2
                tcb, cvb, tb = tcb2[jj], cvb2[jj], tb2[jj]
                B_tc, B_cv, B_tb = B_tc2[jj], B_cv2[jj], B_tb2[jj]
                ACTF(tcb[:, 0:N], psc[:, 0:N], AF.Copy, [bpsc], [B_tc])
                TT("dve", cvb[:, 0:N], psv[:, 0:N], tcb[:, 0:N], ALU.mult, [bpsv, B_tc], [B_cv])
                ACTF(tb[:, 0:V], cvb[:, 2:N], AF.Copy, [B_cv, B_sm["cw"]], [B_tb], scale=cwv[:, 2, j:j + 1])
                ACTF(ugT[:, j, :, c0:c1].rearrange("p k c -> p c k"), pu[:, 2:N].rearrange("p (c k) -> p c k", k=8),
                     AF.Copy, [bpu], B_ug[8 * j:8 * j + 8])
                STT("dve", tb[:, 0:V], cvb[:, 1:N - 1], cwv[:, 1, j:j + 1], tb[:, 0:V], ALU.mult, ALU.add, [B_cv, B_tb, B_sm["cw"]], [B_tb])
                STT("dve", tb[:, 0:V], cvb[:, 0:N - 2], cwv[:, 0, j:j + 1], tb[:, 0:V], ALU.mult, ALU.add, [B_cv, B_tb, B_sm["cw"]], [B_tb])
                TT("dve", co[:, j, 0:V], psb[:, 2:N], tb[:, 0:V], ALU.mult, [bpsb, B_tb], [B_co])

        def pass_tail_a_stats(ti):
            s0, V = TILES[ti]
            ACTF(sqb[:, 0:4, 0:V], co[:, :, 0:V], AF.Square, [B_co], [B_sqb])
            rms_stats(V, 4, lambda k: sqb[:, k, 0:V], B_sqb, 1.0 / 512, sqrt_t, sqrt_t, B_sqrt, B_sqrt)

        def pass_tail_a_cobf(ti):
            s0, V = TILES[ti]
            for j in range(4):
                STT("dve", cobf[:, j, 0:V], co[:, j, 0:V], sm["gainc"][:, j:j + 1], sqrt_t[:, 0:V],
                    ALU.mult, ALU.mult, [B_co, B_sm["gainc"], B_sqrt], [B_cobf])

        def pass_tail_b(ti):
            s0, V = TILES[ti]
            bh = B_h[ti]
            for o in range(KD):
                ps, bps = ps_acc.get()
                for k in range(4):
                    MM(ps[:, 0:V], w_outh[:, k, 128 * o:128 * (o + 1)], cobf[:, k, 0:V], k == 0, k == 3,
                       [B_wouth, B_cobf], [bps], sig=(k == 3))
                TT("dve", hT[:, o, s0 + 2:s0 + 2 + V], hT[:, o, s0 + 2:s0 + 2 + V], ps[:, 0:V], ALU.add, [bps, bh], [bh])

        pass_prologue_stats(0)
        pass_prologue_xbf(0)
        for ti in range(5):
            for j in range(4):
                pass_unit(ti, j)
                if j == 0 and ti + 1 < 5:
                    pass_prologue_sq(ti + 1)
                if j == 1:
                    if ti > 0:
                        pass_tail_b(ti - 1)
                    if ti == 1:
                        ssm_params_part1()
                if j == 2 and ti + 1 < 5:
                    pass_prologue_mm(ti + 1)
            pass_tail_a_stats(ti)
            if ti + 1 < 5:
                pass_prologue_xbf(ti + 1)
            pass_tail_a_cobf(ti)
        pass_tail_b(4)

        if dbg:
            for j in range(4):
                DMA("sp", dbg_d["uT"][:, j * 8 * NCH:(j + 1) * 8 * NCH], ugT[:, j, :, :].rearrange("p k c -> p (k c)"),
                    B_ug[8 * j:8 * j + 8], [])
            DMA("sp", dbg_d["pr"], PRa.rearrange("p g j -> p (g j)"), [B_PR], [])
            DMA("sp", dbg_d["pi"], PIa.rearrange("p g j -> p (g j)"), [B_PI], [])

        P.barrier()
        DMA("pool", w_outh, w_out_d[:, 4:8, :], [], [B_wouth])

        B_slot = {n: [Buf(), Buf()] for n in "Wf Qf sT2 L1 L2 toet M1 M2 X1 X2 pA pB".split()}
        B_slot3 = {n: [Buf(), Buf(), Buf()] for n in "Toe R1 R2".split()}
        B_ES, B_EC, B_PH, B_PT = [Buf(), Buf()], [Buf(), Buf()], Buf(), Buf()
        psS = [(banks[0], bbuf[0]), (banks[1], bbuf[1])]
        psS1 = [(banks[2], bbuf[2]), (banks[3], bbuf[3])]
        psS2 = [(banks[4], bbuf[4]), (banks[5], bbuf[5])]
        psY = [(banks[6], bbuf[6]), (banks[7], bbuf[7])]
        for s in range(2):
            P.op("pool", (lambda s: lambda e: e.memset(X1b[s][:, 0:1], 0.0))(s), [], [B_slot["X1"][s]])
            P.op("pool", (lambda s: lambda e: e.memset(X2b[s][:, 0:1], 0.0))(s), [], [B_slot["X2"][s]])
        B_scrU = [Buf() for _ in range(4)]
        B_scrG = [Buf() for _ in range(4)]
        for j in range(4):
            DMA("sp", scrU[j], ugT[:, j, :, :], B_ug[8 * j:8 * j + 8], [B_scrU[j]])
        for j in range(4):
            for kp in range(8):
                src = scrU[j, :, kp, :].rearrange("(g h) c -> h g c", h=16)
                DMA("sp", UG[16 * kp:16 * kp + 16, 8 * j:8 * j + 8, :], src, [B_scrU[j]], B_UG[8 * j:8 * j + 8], join=(kp > 0))
        identv, maskv = sm["ident"][:], sm["mask"][:]
        SE = "dve" if OPT_SSM else "pool"
        cidx3 = sm["cidx"][:].unsqueeze(1).to_broadcast([128, 2, NCH])
        W3 = lambda t: t.rearrange("p (a b) -> p a b", a=8)
        tab_slot = {}

        def slots(g):
            s, s3 = g % 2, g % 3
            bs = {n: B_slot[n][s] for n in B_slot}
            bs.update({n: B_slot3[n][s3] for n in B_slot3})
            return s, s3, bs

        def tables_act1(g):
            tsl = (g // 2) % 2
            tab_slot[g] = tab_slot[g + 1] = tsl
            for q in range(2):
                ACTF(PH[:, q, :], sm["cidx"][:], AF.Copy, [B_sm["cidx"], bt], [B_PH], scale=t_f8[:, g + q:g + q + 1])
            ACTF(PT, PH, AF.Identity, [B_PH], [B_PT], bias=MAGIC)
            ACTF(PT, PT, AF.Identity, [B_PT], [B_PT], bias=-MAGIC)

        def tables_dve(g):
            TT("dve", PH, PH, PT, ALU.subtract, [B_PH, B_PT], [B_PH])

        def tables_act(g):
            tsl = tab_slot[g]
            ACTF(ES[tsl], PH, AF.Sin, [B_PH], [B_ES[tsl]], scale=TWO_PI_S)
            ACTF(PT, PH, AF.Abs, [B_PH], [B_PT])
            ACTF(EC[tsl], PT, AF.Sin, [B_PT], [B_EC[tsl]], scale=-TWO_PI_S, bias=TWO_PI_S / 4.0)

        def bc_k(t, g, lo):
            return t[:, g, lo:lo + 8].unsqueeze(2).to_broadcast([128, 8, 16])

        def bc_h(t, g):
            return t[:, g, :].unsqueeze(1).to_broadcast([128, 8, 16])

        def setup_dve_wq(g):
            s, s3, bs = slots(g)
            TT("dve", W3(Wf[s]), bc_k(PRa, g, 0), bc_h(BA, g), ALU.mult, [B_PR, B_BA], [bs["Wf"]])
            TT("dve", W3(Qf[s]), bc_k(PRa, g, 16), bc_h(NSCX1, g), ALU.mult, [B_PR, B_SCX1], [bs["Qf"]])
            TT("dve", W3(sT2[s]), bc_k(PIa, g, 0), bc_h(SBB, g), ALU.mult, [B_PI, B_SBB], [bs["sT2"]])
            TT("dve", W3(toet[s]), bc_k(PIa, g, 16), bc_h(NCX2, g), ALU.mult, [B_PI, B_SCX1], [bs["toet"]])
            TT("dve", Wf[s], Wf[s], sT2[s], ALU.add, [bs["Wf"], bs["sT2"]], [bs["Wf"]])
            TT("dve", Qf[s], Qf[s], toet[s], ALU.add, [bs["Qf"], bs["toet"]], [bs["Qf"]])

        def setup_pool_r(g):
            s, s3, bs = slots(g)
            TT("pool", W3(pA[s]), bc_k(PRa, g, 8), bc_h(NSCX1, g), ALU.mult, [B_PR, B_SCX1], [bs["pA"]])
            TT("pool", W3(pB[s]), bc_k(PIa, g, 8), bc_h(NCX2, g), ALU.mult, [B_PI, B_SCX1], [bs["pB"]])
            TT("pool", R1[s3], pA[s], pB[s], ALU.add, [bs["pA"], bs["pB"]], [bs["R1"]])
            TT("pool", W3(pA[s]), bc_k(PIa, g, 8), bc_h(SCX1, g), ALU.mult, [B_PI, B_SCX1], [bs["pA"]])
            TT("pool", W3(pB[s]), bc_k(PRa, g, 8), bc_h(NCX2, g), ALU.mult, [B_PR, B_SCX1], [bs["pB"]])
            TT("pool", R2[s3], pA[s], pB[s], ALU.add, [bs["pA"], bs["pB"]], [bs["R2"]])

        def setup_pe(g):
            s, s3, bs = slots(g)
            pS, bpS = psS[s]
            P.op("pe", (lambda pS, s: lambda e: e.transpose(pS[:, 0:128], Wf[s], identv))(pS, s), [bs["Wf"], B_sm["ident"]], [bpS], sig=False)
            MM(pS[:, 128:256], Wf[s], Qf[s], True, True, [bs["Wf"], bs["Qf"]], [bpS], sig=True)

        def setup_act_l(g):
            s, s3, bs = slots(g)
            pS, bpS = psS[s]
            ACTF(L1[s], pS[:, 0:128], AF.Copy, [bpS], [bs["L1"]])
            ACTF(L2[s][:, 0:64], pS[:, 64:128], AF.Copy, [bpS], [bs["L2"]])
            ACTF(L2[s][:, 64:128], pS[:, 0:64], AF.Copy, [bpS], [bs["L2"]], scale=-1.0)

        def setup_dve_toe(g):
            s, s3, bs = slots(g)
            pS, bpS = psS[s]
            TT("dve", toet[s], pS[:, 128:256], maskv, ALU.mult, [bpS, B_sm["mask"], bs["L1"], bs["L2"]], [bs["toet"]])
            STT("dve", Toe[s3], identv, sm["Dv"][:, g:g + 1], toet[s], ALU.mult, ALU.add,
                [B_sm["ident"], B_sm["Dv"], bs["toet"]], [bs["Toe"]])

        def tabs(g):
            tsl = tab_slot[g]
            return EC[tsl][:, g % 2, :], ES[tsl][:, g % 2, :], B_EC[tsl], B_ES[tsl]

        def main_a_pe(g):
            s, s3, bs = slots(g)
            Ug = UG[:, g, :]
            MM(psS1[s][0][:, 0:NCH], L1[s], Ug, True, True, [bs["L1"], B_UG[g]], [psS1[s][1]], sig=True)
            MM(psS2[s][0][:, 0:NCH], L2[s], Ug, True, True, [bs["L2"], B_UG[g]], [psS2[s][1]], sig=True)

        def main_a_mod(g):
            s, s3, bs = slots(g)
            ec, es, bec, bes = tabs(g)
            TT("dve", M1[s], psS1[s][0][:, 0:NCH], ec, ALU.mult, [psS1[s][1], bec], [bs["M1"]])
            TT("dve", M2[s], psS2[s][0][:, 0:NCH], es, ALU.mult, [psS2[s][1], bes], [bs["M2"]])
            TT("pool", M1[s], M1[s], M2[s], ALU.add, [bs["M1"], bs["M2"]], [bs["M1"]])

        def main_a_scan(g):
            s, s3, bs = slots(g)
            ec, es, bec, bes = tabs(g)
            P.op("dve", (lambda s, g: lambda e: e.tensor_tensor_scan(
                out=M2[s], data0=t_r8[:, g:g + 1].to_broadcast([128, NCH]), data1=M1[s], initial=0.0,
                op0=ALU.mult, op1=ALU.add))(s, g), [bs["M1"], bt], [bs["M2"]])
            TT("dve", X1b[s][:, 1:NCH], M2[s][:, 0:NCH - 1], ec[:, 0:NCH - 1], ALU.mult, [bs["M2"], bec], [bs["X1"]])
            TT("dve", X2b[s][:, 1:NCH], M2[s][:, 0:NCH - 1], es[:, 0:NCH - 1], ALU.mult, [bs["M2"], bes], [bs["X2"]])

        def main_b(g):
            s, s3, bs = slots(g)
            Ug = UG[:, g, :]
            pY, bpY = psY[s]
            MM(pY[:, 0:NCH], Toe[s3], Ug, True, False, [bs["Toe"], B_UG[g]], [bpY])
            MM(pY[:, 0:NCH], R1[s3], X1b[s][:, 0:NCH], False, False, [bs["R1"], bs["X1"]], [bpY])
            MM(pY[:, 0:NCH], R2[s3], X2b[s][:, 0:NCH], False, True, [bs["R2"], bs["X2"]], [bpY], sig=True)
            ACTF(Ug, pY[:, 0:NCH], AF.Gelu, [bpY], [B_UG[g]])
            if g % 8 == 7:
                j = g // 8
                for tau in range(8):
                    dst = scrG[j, :, tau, :].rearrange("(g h) c -> h g c", h=16)
                    DMA("sp", dst, UG[16 * tau:16 * tau + 16, 8 * j:8 * j + 8, :], B_UG[8 * j:8 * j + 8], [B_scrG[j]], join=(tau > 0))
                DMA("sp", ugT[:, j, :, :], scrG[j], [B_scrG[j]], B_ug[8 * j:8 * j + 8])

        tables_act1(0)
        tables_dve(0)
        tables_act(0)
        for step in range(32 + 2):
            g0, g1, g2 = step, step - 1, step - 2
            v0, v1, v2 = g0 < 32, 0 <= g1 < 32, 0 <= g2 < 32
            gen1 = (step % 2 == 1) and (step + 1 < 32)
            gen2 = (step % 2 == 0) and (2 <= step < 32)
            if v2:
                main_b(g2)
            if v1:
                main_a_pe(g1)
            if v0:
                setup_dve_wq(g0)
                setup_pool_r(g0)
                setup_pe(g0)
            if v1:
                main_a_mod(g1)
            if v0:
                setup_act_l(g0)
            if gen1:
                tables_act1(step + 1)
            if gen2:
                tables_act(step)
            if v1:
                setup_dve_toe(g1)
                main_a_scan(g1)
            if gen1:
                tables_dve(step + 1)

        if dbg:
            for j in range(4):
                DMA("sp", dbg_d["gT"][:, j * 8 * NCH:(j + 1) * 8 * NCH], ugT[:, j, :, :].rearrange("p k c -> p (k c)"),
                    B_ug[8 * j:8 * j + 8], [])

        P.barrier()
        B_wup = [Buf(), Buf(), Buf()]
        DMA("pool", wup[:, 0, :, :], w_up_d[0], [], [B_wup[0]])
        DMA("pool", wup[:, 1, :, :], w_up_d[1], [], [B_wup[1]])

        ps_rot = Rot([(banks[i], bbuf[i]) for i in range(4)])
        ps_acc = Rot([(banks[4], bbuf[4]), (banks[5], bbuf[5])])
        ps_stat = (banks[6], bbuf[6])
        B_hn = [Buf(f"hn{i}") for i in range(5)]
        B_sqh, B_rstd2, B_sqrt2 = Buf(), Buf(), Buf()
        P.op("pool", lambda e: e.memset(hnT[:, :, 0:2], 0.0), [], [B_hn[0]])
        cob2 = [cobf, cobf_b]
        B_cob2 = [B_cobf, Buf()]
        sg2 = [sgb, tcb2[0]]
        B_sg2 = [B_sg, B_tc2[0]]

        def m2_z(ti, o):
            s0, V = TILES[ti]
            c0, c1 = s0 // 8, (s0 + V) // 8
            gview = lambda k: ugT[:, k, :, c0:c1].rearrange("p k c -> p c k")
            ps, bps = ps_rot.get()
            for k in range(4):
                MM(ps[:, 0:V], w_glu[:, k, 128 * o:128 * (o + 1)], gview(k), k == 0, k == 3, [B_wglu] + B_ug[8 * k:8 * k + 8], [bps], sig=(k == 3))
            sg, bsg = sg2[o % 2], B_sg2[o % 2]
            ACTF(sg[:, 0:V], ps[:, 0:V], AF.Sigmoid, [bps], [bsg])
            TT("dve", co[:, o, 0:V].rearrange("p (c k) -> p c k", k=8), gview(o), sg[:, 0:V].rearrange("p (c k) -> p c k", k=8),
               ALU.mult, B_ug[8 * o:8 * o + 8] + [bsg], [B_co])

        def m2_a_stats(ti):
            s0, V = TILES[ti]
            ACTF(sqb[:, 0:4, 0:V], co[:, :, 0:V], AF.Square, [B_co], [B_sqb])
            rms_stats(V, 4, lambda k: sqb[:, k, 0:V], B_sqb, 1.0 / 512, sqrt_t, sqrt_t, B_sqrt, B_sqrt)

        def m2_a_cobf(ti):
            s0, V = TILES[ti]
            cb, bcb = cob2[ti % 2], B_cob2[ti % 2]
            for j in range(4):
                STT("dve", cb[:, j, 0:V], co[:, j, 0:V], sm["gains"][:, j:j + 1], sqrt_t[:, 0:V],
                    ALU.mult, ALU.mult, [B_co, B_sm["gains"], B_sqrt], [bcb])

        def m2_wout(ti, o):
            s0, V = TILES[ti]
            bh = B_h[ti]
            cb, bcb = cob2[ti % 2], B_cob2[ti % 2]
            ps, bps = ps_acc.get()
            for k in range(4):
                MM(ps[:, 0:V], w_outh[:, k, 128 * o:128 * (o + 1)], cb[:, k, 0:V], k == 0, k == 3,
                   [B_wouth, bcb], [bps], sig=(k == 3))
            TT("dve", hT[:, o, s0 + 2:s0 + 2 + V], hT[:, o, s0 + 2:s0 + 2 + V], ps[:, 0:V], ALU.add, [bps, bh], [bh])

        def m2_b2sq(ti):
            s0, V = TILES[ti]
            bh = B_h[ti]
            ACTF(sqb[:, :, 0:V], hT[:, :, s0 + 2:s0 + 2 + V], AF.Square, [bh], [B_sqb])

        def m2_b2mm(ti):
            s0, V = TILES[ti]
            rms_stats(V, KD, lambda k: sqb[:, k, 0:V], B_sqb, 1.0 / D, rstd, rstd, B_rstd, B_rstd)

        def m2_b2s(ti):
            m2_b2sq(ti)
            m2_b2mm(ti)

        def m2_b2x(ti):
            s0, V = TILES[ti]
            bh = B_h[ti]
            for k in range(KD):
                STT("dve", hnT[:, k, s0 + 2:s0 + 2 + V], hT[:, k, s0 + 2:s0 + 2 + V], gffn[:, k:k + 1],
                    rstd[:, 0:V], ALU.mult, ALU.mult, [bh, B_sm["gffn"], B_rstd], [B_hn[ti]])

        for o in range(4):
            m2_z(0, o)
        m2_a_stats(0)
        m2_a_cobf(0)
        for ti in range(5):
            nxt = ti + 1 < 5
            for o in range(4):
                if nxt:
                    m2_z(ti + 1, o)
                m2_wout(ti, 2 * o)
                m2_wout(ti, 2 * o + 1)
                if o == 0 and ti > 0:
                    m2_b2sq(ti - 1)
                if o == 1 and ti > 0:
                    m2_b2mm(ti - 1)
                if o == 3 and ti > 0:
                    m2_b2x(ti - 1)
            if nxt:
                m2_a_stats(ti + 1)
                m2_a_cobf(ti + 1)
        m2_b2s(4)
        m2_b2x(4)

        if dbg:
            for k in range(KD):
                DMA("sp", dbg_d["h_mix"][:, k, :], hT[:, k, :], B_h, [])

        P.barrier()

        B_wdn = [Buf() for _ in range(10)]
        B_hid = [Buf() for _ in range(10)]
        B_a0, B_v0 = [Buf(), Buf(), Buf()], [Buf(), Buf(), Buf()]
        B_sa = [Buf(), Buf()]
        if OPT_FFN:
            psA = Rot([(banks[0], bbuf[0]), (banks[1], bbuf[1]), (banks[2], bbuf[2])])
            psB = Rot([(banks[3], bbuf[3]), (banks[4], bbuf[4]), (banks[5], bbuf[5])])
            psD = Rot([(banks[6], bbuf[6]), (banks[7], bbuf[7])])
        else:
            psA = Rot([(banks[0], bbuf[0]), (banks[1], bbuf[1])])
            psB = Rot([(banks[2], bbuf[2]), (banks[3], bbuf[3])])
            psD = Rot([(banks[4], bbuf[4]), (banks[5], bbuf[5])])
        fwa = sm["fwa"][:].rearrange("p (t f) -> p t f", t=3)
        fwv = sm["fwv"][:].rearrange("p (t f) -> p t f", t=3)
        fba, fbv = sm["fba"], sm["fbv"]
        rr = [0]

        def load_wup(f):
            DMA("pool", wup[:, f % 3, :, :], w_up_d[f], [], [B_wup[f % 3]])

        def load_wdn(f):
            DMA("pool", wdn[f % 10], w_dn_d[f], [], [B_wdn[f % 10]])

        def up_tile(f):
            ws = f % 3
            hs = f % 10
            for ti, (s0, V) in enumerate(TILES):
                N = V + 2
                bhn = [B_hn[ti]] + ([B_hn[ti - 1]] if ti > 0 else [])
                pa, bpa = psA.get()
                pb, bpb = psB.get()
                for k in range(KD):
                    MM(pa[:, 0:N], wup[:, ws, k, 0:128], hnT[:, k, s0:s0 + N], k == 0, k == KD - 1, [B_wup[ws]] + bhn, [bpa], sig=(k == KD - 1))
                for k in range(KD):
                    MM(pb[:, 0:N], wup[:, ws, k, 128:256], hnT[:, k, s0:s0 + N], k == 0, k == KD - 1, [B_wup[ws]] + bhn, [bpb], sig=(k == KD - 1))
                q = rr[0] % (3 if OPT_FFN else 2)
                rr[0] += 1
                a0, v0 = a0b[q], v0b[q]
                if OPT_FFN2:
                    ACTF(a0[:, 0:V], pa[:, 2:N], AF.Identity, [bpa, B_sm["fwa"], B_sm["fba"]], [B_a0[q]], bias=fba[:, f:f + 1], scale=fwa[:, 2, f:f + 1])
                    ACTF(v0[:, 0:V], pb[:, 2:N], AF.Identity, [bpb, B_sm["fwv"], B_sm["fbv"]], [B_v0[q]], bias=fbv[:, f:f + 1], scale=fwv[:, 2, f:f + 1])
                    STT("dve", a0[:, 0:V], pa[:, 1:N - 1], fwa[:, 1, f:f + 1], a0[:, 0:V], ALU.mult, ALU.add, [bpa, B_a0[q], B_sm["fwa"]], [B_a0[q]])
                    STT("dve", v0[:, 0:V], pb[:, 1:N - 1], fwv[:, 1, f:f + 1], v0[:, 0:V], ALU.mult, ALU.add, [bpb, B_v0[q], B_sm["fwv"]], [B_v0[q]])
                    STT("dve", a0[:, 0:V], pa[:, 0:N - 2], fwa[:, 0, f:f + 1], a0[:, 0:V], ALU.mult, ALU.add, [bpa, B_a0[q], B_sm["fwa"]], [B_a0[q]])
                    STT("dve", v0[:, 0:V], pb[:, 0:N - 2], fwv[:, 0, f:f + 1], v0[:, 0:V], ALU.mult, ALU.add, [bpb, B_v0[q], B_sm["fwv"]], [B_v0[q]])
                    ACTF(a0[:, 0:V], a0[:, 0:V], AF.Silu, [B_a0[q]], [B_a0[q]])
                    TT("pool", hid[hs][:, s0:s0 + V], a0[:, 0:V], v0[:, 0:V], ALU.mult, [B_a0[q], B_v0[q]], [B_hid[hs]])
                else:
                    sa = sab[q]
                    ACTF(a0[:, 0:V], pa[:, 2:N], AF.Identity, [bpa, B_sm["fwa"], B_sm["fba"]], [B_a0[q]], bias=fba[:, f:f + 1], scale=fwa[:, 2, f:f + 1])
                    STT("dve", a0[:, 0:V], pa[:, 1:N - 1], fwa[:, 1, f:f + 1], a0[:, 0:V], ALU.mult, ALU.add, [bpa, B_a0[q], B_sm["fwa"]], [B_a0[q]])
                    STT("dve", a0[:, 0:V], pa[:, 0:N - 2], fwa[:, 0, f:f + 1], a0[:, 0:V], ALU.mult, ALU.add, [bpa, B_a0[q], B_sm["fwa"]], [B_a0[q]])
                    ACTF(sa[:, 0:V], a0[:, 0:V], AF.Silu, [B_a0[q]], [B_sa[q]])
                    ACTF(v0[:, 0:V], pb[:, 2:N], AF.Identity, [bpb, B_sm["fwv"], B_sm["fbv"]], [B_v0[q]], bias=fbv[:, f:f + 1], scale=fwv[:, 2, f:f + 1])
                    STT("dve", v0[:, 0:V], pb[:, 1:N - 1], fwv[:, 1, f:f + 1], v0[:, 0:V], ALU.mult, ALU.add, [bpb, B_v0[q], B_sm["fwv"]], [B_v0[q]])
                    STT("dve", v0[:, 0:V], pb[:, 0:N - 2], fwv[:, 0, f:f + 1], v0[:, 0:V], ALU.mult, ALU.add, [bpb, B_v0[q], B_sm["fwv"]], [B_v0[q]])
                    TT("pool", hid[hs][:, s0:s0 + V], sa[:, 0:V], v0[:, 0:V], ALU.mult, [B_sa[q], B_v0[q]], [B_hid[hs]])

        def final_stats(ti):
            s0, V = TILES[ti]
            bh = B_h[ti]
            ACTF(sqh[:, :, 0:V], hT[:, :, s0 + 2:s0 + 2 + V], AF.Square, [bh], [B_sqh])
            rms_stats(V, KD, lambda k: sqh[:, k, 0:V], B_sqh, 1.0 / D, rstd2, rstd2, B_rstd2, B_rstd2, pss=(banks[ti % 3], bbuf[ti % 3]))

        def final_scale(ti, k):
            s0, V = TILES[ti]
            bh = B_h[ti]
            STT("dve", hT[:, k, s0 + 2:s0 + 2 + V], hT[:, k, s0 + 2:s0 + 2 + V], gfin[:, k:k + 1],
                rstd2[:, 0:V], ALU.mult, ALU.mult, [bh, B_sm["gfin"], B_rstd2], [bh])

        def final_store(ti):
            s0, V = TILES[ti]
            t0 = max(s0, NMETA)
            outs.append(DMA("sp", yT_d[:, :, t0 - NMETA:s0 + V - NMETA], hT[:, :, t0 + 2:s0 + 2 + V], [B_h[ti]], []))

        def down_group(fs, last=False):
            for ti, (s0, V) in enumerate(TILES):
                nk = 0
                for o in range(KD):
                    ps, bps = psD.get()
                    for i, f in enumerate(fs):
                        MM(ps[:, 0:V], wdn[f % 10][:, 128 * o:128 * (o + 1)], hid[f % 10][:, s0:s0 + V], i == 0, i == len(fs) - 1,
                           [B_wdn[f % 10], B_hid[f % 10]], [bps], sig=(i == len(fs) - 1))
                    TT("dve", hT[:, o, s0 + 2:s0 + 2 + V], hT[:, o, s0 + 2:s0 + 2 + V], ps[:, 0:V], ALU.add, [bps, B_h[ti]], [B_h[ti]])
                    if last and ti >= 1:
                        if o == 1:
                            final_stats(ti - 1)
                        if o >= 2:
                            final_scale(ti - 1, nk)
                            nk += 1
                if last and ti >= 1:
                    while nk < KD:
                        final_scale(ti - 1, nk)
                        nk += 1
                    final_store(ti - 1)
            if last:
                ti = len(TILES) - 1
                final_stats(ti)
                for k in range(KD):
                    final_scale(ti, k)
                final_store(ti)

        outs = []
        load_wdn(0)
        for gi, fs in enumerate(FGROUPS):
            for f in fs:
                if f + 2 < NF:
                    load_wup(f + 2)
                if f + 1 < NF:
                    load_wdn(f + 1)
                up_tile(f)
                if gi > 0 and f == fs[0]:
                    down_group(FGROUPS[gi - 1])
        down_group(FGROUPS[-1], last=True)

        fin = P.op("sp", lambda e: None, sig=False)
        fin.deps.extend(P.all_dmas)
        fin.deps.extend(outs)
        P.emit(block)
    return nc


def _kmajor(w):
    K = w.shape[0] // 128
    return np.ascontiguousarray(w.reshape(K, 128, w.shape[1]).transpose(1, 0, 2))


def _percol(v, K):
    return np.ascontiguousarray(v.reshape(K, 128).T)


def _prep_shared(inp):
    f = lambda a: np.asarray(a, dtype=np.float32)
    sh = {}
    sh["w_in"] = _kmajor(f(inp["w_in"])[0])
    sh["w_out"] = _kmajor(f(inp["w_out"])[0])
    sh["w_glu"] = _kmajor(f(inp["ssm_w_glu"])[0])
    wu = _kmajor(f(inp["w_up"])[0])
    wa = wu[:, :, :DFF].reshape(128, KD, NF, 128)
    wv = wu[:, :, DFF:].reshape(128, KD, NF, 128)
    sh["w_up"] = np.ascontiguousarray(np.concatenate([wa, wv], axis=3).transpose(2, 0, 1, 3))
    sh["w_dn"] = np.ascontiguousarray(f(inp["w_down"])[0].reshape(NF, 128, 1024))
    sh["gmix"] = _percol(f(inp["norm_mix_g"])[0], 8)
    sh["gffn"] = _percol(f(inp["norm_ffn_g"])[0], 8)
    sh["gfin"] = _percol(f(inp["norm_final_g"]), 8)
    sh["gainc"] = _percol(f(inp["gain_conv_out"])[0], 4)
    sh["gains"] = _percol(f(inp["gain_ssm_out"])[0], 4)
    cw = f(inp["conv_w"])[0]
    sh["cw"] = np.ascontiguousarray(cw.reshape(3, 4, 128).transpose(2, 0, 1).reshape(128, 12))
    fw = f(inp["ffn_conv_w"])[0]
    sh["fwa"] = np.ascontiguousarray(fw[:, :DFF].reshape(3, NF, 128).transpose(2, 0, 1).reshape(128, 66))
    sh["fwv"] = np.ascontiguousarray(fw[:, DFF:].reshape(3, NF, 128).transpose(2, 0, 1).reshape(128, 66))
    fb = f(inp["ffn_conv_b"])[0]
    sh["fba"] = np.ascontiguousarray(fb[:DFF].reshape(NF, 128).T)
    sh["fbv"] = np.ascontiguousarray(fb[DFF:].reshape(NF, 128).T)
    dup = lambda a: np.ascontiguousarray(np.concatenate([a, a], axis=0))
    stack = lambda a, b: np.ascontiguousarray(np.concatenate([a, b], axis=0))
    sh["lr"] = dup(f(inp["ssm_lam_re"])[0].T)
    sh["li"] = dup(f(inp["ssm_lam_im"])[0].T)
    sh["ldt"] = np.ascontiguousarray(np.broadcast_to(f(inp["ssm_log_dt"])[0][None, :], (128, 32)))
    bre = f(inp["ssm_b_re"])[0].transpose(1, 0, 2).reshape(64, 512)
    bim = f(inp["ssm_b_im"])[0].transpose(1, 0, 2).reshape(64, 512)
    sh["X1"] = stack(bre, bim)
    sh["X2"] = stack(bim, bre)
    cre = f(inp["ssm_c_re"])[0].transpose(2, 0, 1).reshape(64, 512)
    cim = f(inp["ssm_c_im"])[0].transpose(2, 0, 1).reshape(64, 512)
    sh["CX1"] = stack(cre, cim)
    sh["CX2"] = stack(cim, cre)
    dd = f(inp["ssm_d"])[0]
    sh["Dv"] = np.ascontiguousarray(np.tile(dd.T, (8, 1)))
    sh["ident"] = np.eye(128, dtype=np.float32)
    kk = np.arange(128) // 16
    sh["mask"] = (kk[None, :] >= kk[:, None]).astype(np.float32)
    sh["cidx"] = np.ascontiguousarray(np.broadcast_to(np.arange(NCH, dtype=np.float32)[None, :], (128, NCH)))
    sh["jv"] = np.ascontiguousarray(np.broadcast_to(np.array(JV, dtype=np.float32)[None, :], (128, 24)))
    sh["sgn"] = np.where(np.arange(128) < 64, -1.0, 1.0).astype(np.float32)[:, None]
    return sh


def _prep_x(x_b, meta):
    h0 = np.concatenate([meta, x_b], axis=0)
    hT = h0.T.reshape(KD, 128, L).transpose(1, 0, 2)
    out = np.zeros((128, KD, LP), dtype=np.float32)
    out[:, :, 2:] = hT
    return out


_NC_CACHE = {}


def kernel(**inputs):
    x = np.asarray(inputs["x"], dtype=np.float32)
    meta = np.asarray(inputs["meta_tokens"], dtype=np.float32)
    sh = _prep_shared(inputs)
    dbg = bool(inputs.get("_dbg", False)) if isinstance(inputs, dict) else False
    if dbg not in _NC_CACHE:
        _NC_CACHE[dbg] = _build(dbg)
    nc = _NC_CACHE[dbg]
    B = x.shape[0]
    in_maps = []
    for b in range(B):
        m = dict(sh)
        m["xT"] = _prep_x(x[b], meta)
        in_maps.append(m)
    res = run_bass_kernel_spmd(nc, in_maps, core_ids=list(range(B)))
    out = np.empty((B, 2048, D), dtype=np.float32)
    for b in range(B):
        yT = np.asarray(res.results[b]["yT"])
        out[b] = yT.transpose(2, 1, 0).reshape(2048, D)
    if dbg:
        kernel.last = res
    return out
```

```python
import math
OPT_SSM = True
OPT_PASS = True
OPT_FFN = True
OPT_FFN2 = True
OPT_TAB = True
OPT_MAT = True
OPT_MATQ = True
OPT_MATR = True
from contextlib import ExitStack

import numpy as np
import concourse.bass as bass
import concourse.mybir as mybir
from concourse.bass_utils import run_bass_kernel_spmd

F32 = mybir.dt.float32
BF16 = mybir.dt.bfloat16
ALU = mybir.AluOpType
AF = mybir.ActivationFunctionType

D = 1024
L = 2064
NMETA = 16
LP = L + 2
KD = 8
DFF = 2816
NF = 22
NCH = 258
EPS = 1e-6
TILES = [(416 * i, min(416, L - 416 * i)) for i in range(5)]
TWO_PI_S = 2.0 * math.pi * (1.0 - 2e-6)
MAGIC = 12582912.0
JV = [7, 6, 5, 4, 3, 2, 1, 0, 1, 2, 3, 4, 5, 6, 7, 8, -7, -6, -5, -4, -3, -2, -1, 0]
FGROUPS = [list(range(0, 8)), list(range(8, 15)), list(range(15, 22))]


class Buf:
    __slots__ = ("name", "lws", "rd")

    def __init__(self, name=""):
        self.name = name
        self.lws = []
        self.rd = []


class Op:
    __slots__ = ("eng", "fn", "deps", "sig", "idx", "ev", "dma")

    def __init__(self, eng, fn, sig, dma):
        self.eng = eng
        self.fn = fn
        self.deps = []
        self.sig = sig
        self.idx = None
        self.ev = None
        self.dma = dma


class Prog:
    ENGS = ("pe", "act", "dve", "pool", "sp")

    def __init__(self, nc, ndma_sems=24):
        self.nc = nc
        self.ops = {e: [] for e in self.ENGS}
        self.sems = {}
        self.ndma = ndma_sems
        self.dma_ring = {}
        self.dma_cnt = {}
        self.dma_last = {}
        self.dma_i = {}
        self.bar = []
        self.all_dmas = []

    def open(self, stack):
        nc = self.nc
        for e in self.ENGS:
            self.sems[e] = stack.enter_context(nc.semaphore("s_" + e))
        for q in ("sp", "act", "pool"):
            n = self.ndma if q == "sp" else 8
            self.dma_ring[q] = [stack.enter_context(nc.semaphore(f"d_{q}{i}")) for i in range(n)]
            self.dma_cnt[q] = [0] * n
            self.dma_last[q] = [None] * n
            self.dma_i[q] = 0

    def op(self, eng, fn, reads=(), writes=(), sig=True, join=False):
        o = Op(eng, fn, sig, None)
        o.deps.extend(self.bar)
        self._deps(o, reads, writes, join)
        o.idx = len(self.ops[eng])
        self.ops[eng].append(o)
        return o

    def dma(self, q, fn, reads=(), writes=(), join=False):
        n = len(self.dma_ring[q])
        i = self.dma_i[q]
        self.dma_i[q] = (i + 1) % n
        o = Op(q, fn, False, (q, i))
        o.deps.extend(self.bar)
        prev = self.dma_last[q][i]
        if prev is not None:
            o.deps.append(prev)
        self._deps(o, reads, writes, join)
        self.dma_cnt[q][i] += 16
        o.ev = (self.dma_ring[q][i], self.dma_cnt[q][i])
        self.dma_last[q][i] = o
        o.idx = len(self.ops[q])
        self.ops[q].append(o)
        self.all_dmas.append(o)
        return o

    def _deps(self, o, reads, writes, join=False):
        for r in reads:
            o.deps.extend(r.lws)
        for w in writes:
            if join and w.lws and not w.rd:
                continue
            o.deps.extend(w.lws)
            o.deps.extend(w.rd)
        for r in reads:
            r.rd.append(o)
        for w in writes:
            if join and w.lws and not w.rd:
                w.lws.append(o)
            else:
                w.lws = [o]
                w.rd = []

    def barrier(self):
        deps = []
        for e in self.ENGS:
            comp = [o for o in self.ops[e] if o.dma is None]
            if comp:
                comp[-1].sig = True
                deps.append(comp[-1])
        deps.extend(self.all_dmas)
        self.all_dmas = []
        self.bar = deps

    def finalize(self):
        for e in self.ENGS:
            lst = [o for o in self.ops[e] if o.dma is None]
            if lst:
                lst[-1].sig = True
            cnt = 0
            pending = []
            for o in lst:
                pending.append(o)
                if o.sig:
                    cnt += 1
                    for p in pending:
                        p.ev = (self.sems[e], cnt)
                    pending = []

    def emit(self, block):
        self.finalize()

        def run(e):
            def body(eng):
                waited = {}
                for o in self.ops[e]:
                    need = {}
                    for d in o.deps:
                        if d.dma is None and d.eng == e:
                            if e == "pe" and o.dma is None:
                                continue
                            if not d.sig:
                                nxt = [x for x in self.ops[e][d.idx:] if x.dma is None and x.sig][0]
                                if nxt.idx >= o.idx:
                                    continue
                        s, v = d.ev
                        if need.get(s.name, (None, 0))[1] < v:
                            need[s.name] = (s, v)
                    for sn, (s, v) in need.items():
                        if waited.get(sn, 0) < v:
                            eng.wait_ge(s, v)
                            waited[sn] = v
                    ins = o.fn(eng)
                    if ins is None:
                        continue
                    if o.dma is not None:
                        ins.then_inc(o.ev[0], 16)
                    elif o.sig:
                        ins.then_inc(o.ev[0], 1)
            return body

        block.tensor(run("pe"))
        block.scalar(run("act"))
        block.vector(run("dve"))
        block.gpsimd(run("pool"))
        block.sync(run("sp"))


class Rot:
    def __init__(self, items):
        self.items = items
        self.i = 0

    def get(self):
        it = self.items[self.i]
        self.i = (self.i + 1) % len(self.items)
        return it


def _build(dbg=False):
    nc = bass.Bass("TRN2", target_bir_lowering=False)

    def din(name, shape):
        return nc.dram_tensor(name, list(shape), F32, kind="ExternalInput").ap()

    xT_d = din("xT", [128, KD, LP])
    w_in_d = din("w_in", [128, KD, 2048])
    w_out_d = din("w_out", [128, KD, 1024])
    w_glu_d = din("w_glu", [128, 4, 512])
    w_up_d = din("w_up", [NF, 128, KD, 256])
    w_dn_d = din("w_dn", [NF, 128, 1024])
    small_specs = dict(gmix=8, gffn=8, gfin=8, gainc=4, gains=4, cw=12, fwa=66, fwv=66, fba=22, fbv=22,
                       lr=32, li=32, ldt=32, X1=512, X2=512, CX1=512, CX2=512, Dv=32,
                       ident=128, mask=128, cidx=NCH, jv=24, sgn=1)
    small_d = {k: din(k, [128, n]) for k, n in small_specs.items()}
    yT_d = nc.dram_tensor("yT", [128, KD, 2048], F32, kind="ExternalOutput").ap()
    scrU = nc.dram_tensor("scrU", [4, 128, 8, NCH], BF16).ap()
    scrG = nc.dram_tensor("scrG", [4, 128, 8, NCH], BF16).ap()
    dbg_d = {}
    if dbg:
        dbg_d["h_mix"] = nc.dram_tensor("h_mix", [128, KD, LP], F32, kind="ExternalOutput").ap()
        dbg_d["gT"] = nc.dram_tensor("gT", [128, 4 * 8 * NCH], BF16, kind="ExternalOutput").ap()
        dbg_d["uT"] = nc.dram_tensor("uT", [128, 4 * 8 * NCH], BF16, kind="ExternalOutput").ap()
        dbg_d["pr"] = nc.dram_tensor("pr", [128, 32 * 24], F32, kind="ExternalOutput").ap()
        dbg_d["pi"] = nc.dram_tensor("pi", [128, 32 * 24], F32, kind="ExternalOutput").ap()

    with ExitStack() as st:
        def sbt(name, shape, dt):
            return st.enter_context(nc.sbuf_tensor(name, list(shape), dt))

        hT = sbt("hT", [128, KD, LP], F32)
        sm = {k: sbt("sm_" + k, [128, n], F32) for k, n in small_specs.items()}
        ones_bf = sbt("ones_bf", [128, 128], BF16)
        tiny = sbt("tiny", [128, 16, 32], F32)

        ARENA_BYTES = 128 * 1024
        arena = sbt("arena", [128, ARENA_BYTES // 2], BF16)

        class Carver:
            def __init__(self):
                self.off = 0

            def take(self, shape, dt):
                n = int(np.prod(shape))
                nb = n * (4 if dt == F32 else 2)
                nb = (nb + 63) // 64 * 64
                assert self.off + nb <= ARENA_BYTES, (self.off, nb)
                v = arena[:, self.off // 2:(self.off + nb) // 2]
                if dt == F32:
                    v = v.bitcast(F32)
                v = v[:, 0:n]
                if len(shape) == 2:
                    v = v.rearrange("p (a b) -> p a b", a=shape[0], b=shape[1])
                elif len(shape) == 3:
                    v = v.rearrange("p (a b c) -> p a b c", a=shape[0], b=shape[1], c=shape[2])
                self.off += nb
                return v

        cm = Carver()
        w_in = cm.take([KD, 2048], BF16)
        xbf = cm.take([KD, 418], BF16)
        sqb = cm.take([KD, 418], BF16)
        regionA_end = cm.off
        w_outh = cm.take([4, 1024], BF16)
        w_glu = cm.take([4, 512], BF16)
        ugT = cm.take([4, 8, NCH], BF16)
        rstd = cm.take([418], F32)
        sqrt_t = cm.take([418], F32)
        tcb2 = [cm.take([418], F32) for _ in range(2)]
        cvb2 = [cm.take([418], F32) for _ in range(2)]
        tb2 = [cm.take([416], F32) for _ in range(2)]
        co = cm.take([4, 416], F32)
        cobf = cm.take([4, 416], BF16)
        sgb = cm.take([416], F32)
        cobf_b = cm.take([4, 416], BF16)
        cs_ = Carver()
        UG = cs_.take([32, NCH], BF16)
        Wf = [cs_.take([128], F32) for _ in range(2)]
        Qf = [cs_.take([128], F32) for _ in range(2)]
        sT2 = [cs_.take([128], F32) for _ in range(2)]
        L1 = [cs_.take([128], BF16) for _ in range(2)]
        L2 = [cs_.take([128], BF16) for _ in range(2)]
        Toe = [cs_.take([128], BF16) for _ in range(3)]
        R1 = [cs_.take([128], BF16) for _ in range(3)]
        R2 = [cs_.take([128], BF16) for _ in range(3)]
        toet = [cs_.take([128], F32) for _ in range(2)]
        pA = [cs_.take([128], F32) for _ in range(2)]
        pB = [cs_.take([128], F32) for _ in range(2)]
        ES = [cs_.take([2, NCH], F32) for _ in range(2)]
        EC = [cs_.take([2, NCH], F32) for _ in range(2)]
        PH = cs_.take([2, NCH], F32)
        PT = cs_.take([2, NCH], F32)
        M1 = [cs_.take([NCH], F32) for _ in range(2)]
        M2 = [cs_.take([NCH], F32) for _ in range(2)]
        X1b = [cs_.take([NCH + 2], BF16) for _ in range(2)]
        X2b = [cs_.take([NCH + 2], BF16) for _ in range(2)]
        assert cs_.off <= regionA_end, (cs_.off, regionA_end)
        ssm_param_start = cm.off
        T1h = [cm.take([16, 24], F32) for _ in range(2)]
        T2h = [cm.take([16, 24], F32) for _ in range(2)]
        TB1 = cm.take([32, 16], F32)
        TB2 = cm.take([32, 16], F32)
        PRa = cm.take([32, 24], F32)
        PIa = cm.take([32, 24], F32)
        BA = cm.take([32, 16], F32)
        SBB = cm.take([32, 16], F32)
        SCX1 = cm.take([32, 16], F32)
        NSCX1 = cm.take([32, 16], F32)
        NCX2 = cm.take([32, 16], F32)
        mixer_end = cm.off
        cf = Carver()
        hnT = cf.take([KD, LP], BF16)
        assert cf.off <= regionA_end + 8 * 1024 or True
        hid = [cf.take([L], BF16) for _ in range(10)]
        wdn = [cf.take([1024], BF16) for _ in range(10)]
        a0b = [cf.take([416], F32) for _ in range(3)]
        v0b = [cf.take([416], F32) for _ in range(3)]
        sab = [a0b[2], v0b[2]]
        sqh = cf.take([KD, 416], BF16)
        rstd2 = cf.take([416], F32)
        sqrt2 = cf.take([416], F32)
        assert cf.off >= ssm_param_start, (cf.off, ssm_param_start)
        wup = cf.take([3, KD, 256], BF16)
        assert 2 * KD * LP <= 32768 + 2 * KD * 418

        banks = [st.enter_context(nc.psum_tensor(f"ps{i}", [128, 512], F32)) for i in range(8)]
        bbuf = [Buf(f"ps{i}") for i in range(8)]

        P = Prog(nc)
        P.open(st)
        block = st.enter_context(nc.Block())

        def TT(eng, out, in0, in1, op, r, w, sig=True):
            return P.op(eng, lambda e: e.tensor_tensor(out=out, in0=in0, in1=in1, op=op), r, w, sig)

        def TS(eng, out, in0, s1, s2, op0, op1, r, w):
            if s2 is None:
                return P.op(eng, lambda e: e.tensor_scalar(out=out, in0=in0, scalar1=s1, scalar2=None, op0=op0), r, w)
            return P.op(eng, lambda e: e.tensor_scalar(out=out, in0=in0, scalar1=s1, scalar2=s2, op0=op0, op1=op1), r, w)

        def STT(eng, out, in0, sc, in1, op0, op1, r, w):
            return P.op(eng, lambda e: e.scalar_tensor_tensor(out=out, in0=in0, scalar=sc, in1=in1, op0=op0, op1=op1), r, w)

        def ACTF(out, in_, func, r, w, bias=0.0, scale=1.0):
            return P.op("act", lambda e: e.activation(out=out, in_=in_, func=func, bias=bias, scale=scale), r, w)

        def MM(out, lhsT, rhs, start, stop, r, w, sig=False):
            return P.op("pe", lambda e: e.matmul(out, lhsT=lhsT, rhs=rhs, start=start, stop=stop), r, w, sig)

        def DMA(q, out, in_, r, w, join=False):
            return P.dma(q, lambda e: e.dma_start(out=out, in_=in_), r, w, join)

        def round_sub(eng, x, tmp, bx, bt):
            TS(eng, tmp, x, MAGIC, None, ALU.add, None, [bx], [bt])
            TS(eng, tmp, tmp, -MAGIC, None, ALU.add, None, [bt], [bt])
            TT(eng, x, x, tmp, ALU.subtract, [bx, bt], [bx])

        B_h = [Buf(f"h{i}") for i in range(5)]
        B_sm = {k: Buf("sm_" + k) for k in small_specs}
        B_ones = Buf("ones")
        B_win = Buf("w_in")
        B_winj = [Buf(f"w_in{j}") for j in range(4)]
        B_wouth = Buf("w_outh")
        B_wglu = Buf("w_glu")
        B_xbf = Buf("xbf")
        B_sqb = Buf("sqb")
        B_rstd = Buf("rstd")
        B_sqrt = Buf("sqrt")
        B_ug = [Buf(f"ugT{g}") for g in range(32)]
        B_UG = [Buf(f"UG{g}") for g in range(32)]
        B_tc2, B_cv2, B_tb2 = [Buf(), Buf()], [Buf(), Buf()], [Buf(), Buf()]
        B_co, B_cobf, B_sg = Buf(), Buf(), Buf()
        B_PR, B_PI, B_BA, B_SBB, B_SCX1, B_tiny = Buf(), Buf(), Buf(), Buf(), Buf(), Buf()
        B_T1h, B_T2h, B_TB1, B_TB2 = [Buf(), Buf()], [Buf(), Buf()], Buf(), Buf()

        w_in4 = w_in.rearrange("p k (b c) -> p k b c", b=4)
        w_in_d4 = w_in_d.rearrange("p k (b c) -> p k b c", b=4)
        for j in range(4):
            DMA("pool", w_in4[:, :, :, 128 * j:128 * (j + 1)], w_in_d4[:, :, :, 128 * j:128 * (j + 1)], [], [B_winj[j]])
        for ti, (s0, V) in enumerate(TILES):
            lo = 0 if ti == 0 else s0 + 2
            DMA("sp", hT[:, :, lo:s0 + 2 + V], xT_d[:, :, lo:s0 + 2 + V], [B_winj[0]] if ti >= 1 else [], [B_h[ti]])
            if ti == 0:
                first = ["gmix", "cw", "gainc", "ldt", "lr", "li", "jv", "sgn"]
                for k in first + [k for k in small_d if k not in first]:
                    DMA("sp", sm[k][:], small_d[k], [], [B_sm[k]])
        P.op("dve", lambda e: e.memset(ones_bf[:], 1.0), [], [B_ones])

        tn = lambda i: tiny[:, i, :]
        lr, li, ldt = sm["lr"][:], sm["li"][:], sm["ldt"][:]
        sgn = sm["sgn"][:, 0:1]
        t_dt, t_y8, t_th, t_r8, t_f8, t_nr, t_den, t_fr, t_fi, t_a, t_b = [tn(i) for i in range(11)]
        bt = B_tiny
        X1v = sm["X1"][:].rearrange("p (g h) -> p g h", g=32)
        X2v = sm["X2"][:].rearrange("p (g h) -> p g h", g=32)
        CX1v = sm["CX1"][:].rearrange("p (g h) -> p g h", g=32)
        CX2v = sm["CX2"][:].rearrange("p (g h) -> p g h", g=32)

        B_den = Buf()

        def ssm_params_part0():
            TT("pool", t_den, lr, lr, ALU.mult, [B_sm["lr"]], [B_den])
            TT("pool", t_b, li, li, ALU.mult, [B_sm["li"]], [B_den])
            TT("pool", t_den, t_den, t_b, ALU.add, [B_den], [B_den])
            P.op("dve", lambda e: e.reciprocal(out=t_den, in_=t_den), [B_den], [B_den])
            ACTF(t_dt, ldt, AF.Exp, [B_sm["ldt"]], [bt])
            TT("pool", t_y8, lr, t_dt, ALU.mult, [B_sm["lr"], bt], [bt])
            TT("pool", t_th, li, t_dt, ALU.mult, [B_sm["li"], bt], [bt])
            TS("pool", t_th, t_th, 1.0 / (2.0 * math.pi), None, ALU.mult, None, [bt], [bt])
            jv3 = sm["jv"][:].unsqueeze(1).to_broadcast([128, 16, 24])
            for half in range(2):
                gs = slice(16 * half, 16 * half + 16)
                T1, T2, B_T1, B_T2 = T1h[half], T2h[half], B_T1h[half], B_T2h[half]
                y8b = t_y8[:, gs].unsqueeze(2).to_broadcast([128, 16, 24])
                thb = t_th[:, gs].unsqueeze(2).to_broadcast([128, 16, 24])
                TT("pool", T1, jv3, y8b, ALU.mult, [B_sm["jv"], bt], [B_T1])
                TS("pool", T2, T1, 1.0 / 6.0, 1.0, ALU.mult, ALU.add, [B_T1], [B_T2])
                for kk in (5, 4, 3, 2, 1):
                    TT("pool", T2, T2, T1, ALU.mult, [B_T2, B_T1], [B_T2])
                    TS("pool", T2, T2, 1.0 / kk, 1.0, ALU.mult, ALU.add, [B_T2], [B_T2])
                TT("pool", T1, jv3, thb, ALU.mult, [B_sm["jv"], bt], [B_T1])
                round_sub("pool", T1, PRa[:, gs, :], B_T1, B_PR)
                P.op("pool", (lambda gs, T2: lambda e: e.tensor_copy(out=t_r8[:, gs], in_=T2[:, :, 15]))(gs, T2), [B_T2], [bt])
            TS("pool", t_f8, t_th, 8.0, None, ALU.mult, None, [bt], [bt])
            round_sub("pool", t_f8, t_a, bt, bt)
            TS("pool", SCX1, CX1v, sgn, None, ALU.mult, None, [B_sm["CX1"], B_sm["sgn"]], [B_SCX1])
            TS("pool", NSCX1, SCX1, -1.0, None, ALU.mult, None, [B_SCX1], [B_SCX1])
            TS("pool", NCX2, CX2v, -1.0, None, ALU.mult, None, [B_sm["CX2"]], [B_SCX1])

        def ssm_params_part1():
            for half in range(2):
                gs = slice(16 * half, 16 * half + 16)
                T1, T2, B_T1, B_T2 = T1h[half], T2h[half], B_T1h[half], B_T2h[half]
                ACTF(PIa[:, gs, :], T1, AF.Sin, [B_T1], [B_PI], scale=TWO_PI_S)
                ACTF(T1, T1, AF.Abs, [B_T1], [B_T1])
                ACTF(PRa[:, gs, :], T1, AF.Sin, [B_T1], [B_PR], scale=-TWO_PI_S, bias=TWO_PI_S / 4.0)
                TT("pool", PIa[:, gs, :], PIa[:, gs, :], T2, ALU.mult, [B_PI, B_T2], [B_PI])
                TT("pool", PRa[:, gs, :], PRa[:, gs, :], T2, ALU.mult, [B_PR, B_T2], [B_PR])
            ar = PRa[:, :, 6]
            ai = PIa[:, :, 6]
            TS("pool", t_nr, ar, -1.0, None, ALU.add, None, [B_PR], [bt])
            TT("pool", t_fr, t_nr, lr, ALU.mult, [bt, B_sm["lr"]], [bt])
            TT("pool", t_a, ai, li, ALU.mult, [B_PI, B_sm["li"]], [bt])
            TT("pool", t_fr, t_fr, t_a, ALU.add, [bt], [bt])
            TT("pool", t_fr, t_fr, t_den, ALU.mult, [bt, B_den], [bt])
            TT("pool", t_fi, ai, lr, ALU.mult, [B_PI, B_sm["lr"]], [bt])
            TT("pool", t_a, t_nr, li, ALU.mult, [bt, B_sm["li"]], [bt])
            TT("pool", t_fi, t_fi, t_a, ALU.subtract, [bt], [bt])
            TT("pool", t_fi, t_fi, t_den, ALU.mult, [bt, B_den], [bt])
            TS("pool", t_fi, t_fi, sgn, None, ALU.mult, None, [bt, B_sm["sgn"]], [bt])
            frb = t_fr.unsqueeze(2).to_broadcast([128, 32, 16])
            fib = t_fi.unsqueeze(2).to_broadcast([128, 32, 16])
            TT("pool", TB1, frb, X1v, ALU.mult, [bt, B_sm["X1"]], [B_TB1])
            TT("pool", TB2, fib, X2v, ALU.mult, [bt, B_sm["X2"]], [B_TB2])
            TT("pool", BA, TB1, TB2, ALU.add, [B_TB1, B_TB2], [B_BA])
            TT("pool", TB1, frb, X2v, ALU.mult, [bt, B_sm["X2"]], [B_TB1])
            TT("pool", TB2, fib, X1v, ALU.mult, [bt, B_sm["X1"]], [B_TB2])
            TT("pool", SBB, TB1, TB2, ALU.subtract, [B_TB1, B_TB2], [B_SBB])
            TS("pool", SBB, SBB, sgn, None, ALU.mult, None, [B_SBB, B_sm["sgn"]], [B_SBB])

        DMA("pool", w_outh, w_out_d[:, 0:4, :], [], [B_wouth])
        DMA("pool", w_glu, w_glu_d, [], [B_wglu])

        ps_rot = Rot([(banks[i], bbuf[i]) for i in range(6)])
        ps_stat = (banks[6], bbuf[6])
        ps_acc = Rot([(banks[7], bbuf[7]), (banks[6], bbuf[6])])
        gmix, gffn, gfin = sm["gmix"], sm["gffn"], sm["gfin"]
        cwv = sm["cw"][:].rearrange("p (t j) -> p t j", t=3)

        def rms_stats(n, nk, src_sq, bsq, scale, out_rstd, out_sqrt, brstd, bsqrt, pss=None):
            ps, bps = pss if pss is not None else ps_stat
            for k in range(nk):
                MM(ps[:, 0:n], ones_bf[:], src_sq(k), k == 0, k == nk - 1, [B_ones, bsq], [bps], sig=(k == nk - 1))
            ACTF(out_sqrt[:, 0:n], ps[:, 0:n], AF.Ln, [bps], [bsqrt], bias=EPS, scale=scale)
            ACTF(out_rstd[:, 0:n], out_sqrt[:, 0:n], AF.Exp, [bsqrt], [brstd], scale=-0.5)

        def pass_prologue_sq(ti):
            s0, V = TILES[ti]
            bh = B_h[ti]
            ACTF(sqb[:, :, 0:V], hT[:, :, s0 + 2:s0 + 2 + V], AF.Square, [bh], [B_sqb])

        def pass_prologue_mm(ti):
            s0, V = TILES[ti]
            rms_stats(V, KD, lambda k: sqb[:, k, 0:V], B_sqb, 1.0 / D, rstd, rstd, B_rstd, B_rstd)

        def pass_prologue_stats(ti):
            pass_prologue_sq(ti)
            pass_prologue_mm(ti)

        def pass_prologue_xbf(ti):
            s0, V = TILES[ti]
            N = V + 2
            bh = B_h[ti]
            if ti == 0:
                P.op("dve", lambda e: e.memset(xbf[:, :, 0:2], 0.0), [], [B_xbf])
            else:
                Np = TILES[ti - 1][1] + 2
                P.op("dve", (lambda Np: lambda e: e.tensor_copy(out=xbf[:, :, 0:2], in_=xbf[:, :, Np - 2:Np]))(Np), [B_xbf], [B_xbf])
            for k in range(KD):
                STT("dve", xbf[:, k, 2:N], hT[:, k, s0 + 2:s0 + 2 + V], gmix[:, k:k + 1], rstd[:, 0:V],
                    ALU.mult, ALU.mult, [bh, B_sm["gmix"], B_rstd], [B_xbf])

        def pass_unit(ti, j):
            s0, V = TILES[ti]
            N = V + 2
            c0, c1 = s0 // 8, (s0 + V) // 8
            if True:
                ps, bps = ps_rot.get()
                for k in range(KD):
                    MM(ps[:, 0:N], w_in[:, k, 1536 + 128 * j:1536 + 128 * (j + 1)], xbf[:, k, 0:N], k == 0, k == KD - 1,
                       [B_winj[j], B_xbf], [bps], sig=(k == KD - 1))
                pu, bpu = ps, bps
                psc, bpsc = ps_rot.get()
                psv, bpsv = ps_rot.get()
                psb, bpsb = ps_rot.get()
                for (ps, bps, col) in ((psc, bpsc, 512 + 128 * j), (psv, bpsv, 1024 + 128 * j), (psb, bpsb, 128 * j)):
                    for k in range(KD):
                        MM(ps[:, 0:N], w_in[:, k, col:col + 128], xbf[:, k, 0:N], k == 0, k == KD - 1,
                           [B_winj[j], B_xbf], [bps], sig=(k == KD - 1))
                jj = j % 2
                tcb, cvb, tb = tcb2[jj], cvb2[jj], tb2[jj]
                B_tc, B_cv, B_tb = B_tc2[jj], B_cv2[jj], B_tb2[jj]
                ACTF(tcb[:, 0:N], psc[:, 0:N], AF.Copy, [bpsc], [B_tc])
                TT("dve", cvb[:, 0:N], psv[:, 0:N], tcb[:, 0:N], ALU.mult, [bpsv, B_tc], [B_cv])
                ACTF(tb[:, 0:V], cvb[:, 2:N], AF.Copy, [B_cv, B_sm["cw"]], [B_tb], scale=cwv[:, 2, j:j + 1])
                ACTF(ugT[:, j, :, c0:c1].rearrange("p k c -> p c k"), pu[:, 2:N].rearrange("p (c k) -> p c k", k=8),
                     AF.Copy, [bpu], B_ug[8 * j:8 * j + 8])
                STT("dve", tb[:, 0:V], cvb[:, 1:N - 1], cwv[:, 1, j:j + 1], tb[:, 0:V], ALU.mult, ALU.add, [B_cv, B_tb, B_sm["cw"]], [B_tb])
                STT("dve", tb[:, 0:V], cvb[:, 0:N - 2], cwv[:, 0, j:j + 1], tb[:, 0:V], ALU.mult, ALU.add, [B_cv, B_tb, B_sm["cw"]], [B_tb])
                TT("dve", co[:, j, 0:V], psb[:, 2:N], tb[:, 0:V], ALU.mult, [bpsb, B_tb], [B_co])

        def pass_tail_a_stats(ti):
            s0, V = TILES[ti]
            ACTF(sqb[:, 0:4, 0:V], co[:, :, 0:V], AF.Square, [B_co], [B_sqb])
            rms_stats(V, 4, lambda k: sqb[:, k, 0:V], B_sqb, 1.0 / 512, sqrt_t, sqrt_t, B_sqrt, B_sqrt)

        def pass_tail_a_cobf(ti):
            s0, V = TILES[ti]
            for j in range(4):
                STT("dve", cobf[:, j, 0:V], co[:, j, 0:V], sm["gainc"][:, j:j + 1], sqrt_t[:, 0:V],
                    ALU.mult, ALU.mult, [B_co, B_sm["gainc"], B_sqrt], [B_cobf])

        def pass_tail_b(ti):
            s0, V = TILES[ti]
            bh = B_h[ti]
            for o in range(KD):
                ps, bps = ps_acc.get()
                for k in range(4):
                    MM(ps[:, 0:V], w_outh[:, k, 128 * o:128 * (o + 1)], cobf[:, k, 0:V], k == 0, k == 3,
                       [B_wouth, B_cobf], [bps], sig=(k == 3))
                TT("dve", hT[:, o, s0 + 2:s0 + 2 + V], hT[:, o, s0 + 2:s0 + 2 + V], ps[:, 0:V], ALU.add, [bps, bh], [bh])

        pass_prologue_stats(0)
        pass_prologue_xbf(0)
        P.op("pool", lambda e: e.memset(tiny[:, 15, 0:1], 0.0), [B_xbf], [Buf()])
        ssm_params_part0()
        for ti in range(5):
            for j in range(4):
                pass_unit(ti, j)
                if j == 0 and ti + 1 < 5:
                    pass_prologue_sq(ti + 1)
                if j == 1:
                    if ti > 0:
                        pass_tail_b(ti - 1)
                    if ti == 2:
                        ssm_params_part1()
                if j == 2 and ti + 1 < 5:
                    pass_prologue_mm(ti + 1)
            pass_tail_a_stats(ti)
            if ti + 1 < 5:
                pass_prologue_xbf(ti + 1)
            pass_tail_a_cobf(ti)
        pass_tail_b(4)

        if dbg:
            for j in range(4):
                DMA("sp", dbg_d["uT"][:, j * 8 * NCH:(j + 1) * 8 * NCH], ugT[:, j, :, :].rearrange("p k c -> p (k c)"),
                    B_ug[8 * j:8 * j + 8], [])
            DMA("sp", dbg_d["pr"], PRa.rearrange("p g j -> p (g j)"), [B_PR], [])
            DMA("sp", dbg_d["pi"], PIa.rearrange("p g j -> p (g j)"), [B_PI], [])

        P.barrier()
        DMA("pool", w_outh, w_out_d[:, 4:8, :], [], [B_wouth])

        B_slot = {n: [Buf(), Buf()] for n in "Wf Qf sT2 L1 L2 toet M1 M2 X1 X2 pA pB".split()}
        B_slot3 = {n: [Buf(), Buf(), Buf()] for n in "Toe R1 R2".split()}
        B_ES, B_EC, B_PH, B_PT = [Buf(), Buf()], [Buf(), Buf()], Buf(), Buf()
        psS = [(banks[0], bbuf[0]), (banks[1], bbuf[1])]
        psS1 = [(banks[2], bbuf[2]), (banks[3], bbuf[3])]
        psS2 = [(banks[4], bbuf[4]), (banks[5], bbuf[5])]
        psY = [(banks[6], bbuf[6]), (banks[7], bbuf[7])]
        for s in range(2):
            P.op("pool", (lambda s: lambda e: e.memset(X1b[s][:, 0:1], 0.0))(s), [], [B_slot["X1"][s]])
            P.op("pool", (lambda s: lambda e: e.memset(X2b[s][:, 0:1], 0.0))(s), [], [B_slot["X2"][s]])
        B_scrU = [Buf() for _ in range(4)]
        B_scrG = [Buf() for _ in range(4)]
        for j in range(4):
            DMA("sp", scrU[j], ugT[:, j, :, :], B_ug[8 * j:8 * j + 8], [B_scrU[j]])
        for j in range(4):
            for kp in range(8):
                src = scrU[j, :, kp, :].rearrange("(g h) c -> h g c", h=16)
                DMA("sp", UG[16 * kp:16 * kp + 16, 8 * j:8 * j + 8, :], src, [B_scrU[j]], B_UG[8 * j:8 * j + 8], join=(kp > 0))
        identv, maskv = sm["ident"][:], sm["mask"][:]
        SE = "dve" if OPT_SSM else "pool"
        cidx3 = sm["cidx"][:].unsqueeze(1).to_broadcast([128, 2, NCH])
        W3 = lambda t: t.rearrange("p (a b) -> p a b", a=8)
        tab_slot = {}

        def slots(g):
            s, s3 = g % 2, g % 3
            bs = {n: B_slot[n][s] for n in B_slot}
            bs.update({n: B_slot3[n][s3] for n in B_slot3})
            return s, s3, bs

        def tables_act1(g):
            tsl = (g // 2) % 2
            tab_slot[g] = tab_slot[g + 1] = tsl
            for q in range(2):
                ACTF(PH[:, q, :], sm["cidx"][:], AF.Copy, [B_sm["cidx"], bt], [B_PH], scale=t_f8[:, g + q:g + q + 1])
            ACTF(PT, PH, AF.Identity, [B_PH], [B_PT], bias=MAGIC)
            ACTF(PT, PT, AF.Identity, [B_PT], [B_PT], bias=-MAGIC)

        def tables_dve(g):
            TT("dve", PH, PH, PT, ALU.subtract, [B_PH, B_PT], [B_PH])

        def tables_act(g):
            tsl = tab_slot[g]
            ACTF(ES[tsl], PH, AF.Sin, [B_PH], [B_ES[tsl]], scale=TWO_PI_S)
            ACTF(PT, PH, AF.Abs, [B_PH], [B_PT])
            ACTF(EC[tsl], PT, AF.Sin, [B_PT], [B_EC[tsl]], scale=-TWO_PI_S, bias=TWO_PI_S / 4.0)

        def bc_k(t, g, lo):
            return t[:, g, lo:lo + 8].unsqueeze(2).to_broadcast([128, 8, 16])

        def bc_h(t, g):
            return t[:, g, :].unsqueeze(1).to_broadcast([128, 8, 16])

        def setup_dve_wq(g):
            s, s3, bs = slots(g)
            TT("dve", W3(Wf[s]), bc_k(PRa, g, 0), bc_h(BA, g), ALU.mult, [B_PR, B_BA], [bs["Wf"]])
            TT("dve", W3(Qf[s]), bc_k(PRa, g, 16), bc_h(NSCX1, g), ALU.mult, [B_PR, B_SCX1], [bs["Qf"]])
            TT("dve", W3(sT2[s]), bc_k(PIa, g, 0), bc_h(SBB, g), ALU.mult, [B_PI, B_SBB], [bs["sT2"]])
            TT("dve", W3(toet[s]), bc_k(PIa, g, 16), bc_h(NCX2, g), ALU.mult, [B_PI, B_SCX1], [bs["toet"]])
            TT("dve", Wf[s], Wf[s], sT2[s], ALU.add, [bs["Wf"], bs["sT2"]], [bs["Wf"]])
            TT("dve", Qf[s], Qf[s], toet[s], ALU.add, [bs["Qf"], bs["toet"]], [bs["Qf"]])

        def setup_pool_r(g):
            s, s3, bs = slots(g)
            TT("pool", W3(pA[s]), bc_k(PRa, g, 8), bc_h(NSCX1, g), ALU.mult, [B_PR, B_SCX1], [bs["pA"]])
            TT("pool", W3(pB[s]), bc_k(PIa, g, 8), bc_h(NCX2, g), ALU.mult, [B_PI, B_SCX1], [bs["pB"]])
            TT("pool", R1[s3], pA[s], pB[s], ALU.add, [bs["pA"], bs["pB"]], [bs["R1"]])
            TT("pool", W3(pA[s]), bc_k(PIa, g, 8), bc_h(SCX1, g), ALU.mult, [B_PI, B_SCX1], [bs["pA"]])
            TT("pool", W3(pB[s]), bc_k(PRa, g, 8), bc_h(NCX2, g), ALU.mult, [B_PR, B_SCX1], [bs["pB"]])
            TT("pool", R2[s3], pA[s], pB[s], ALU.add, [bs["pA"], bs["pB"]], [bs["R2"]])

        def setup_pe(g):
            s, s3, bs = slots(g)
            pS, bpS = psS[s]
            P.op("pe", (lambda pS, s: lambda e: e.transpose(pS[:, 0:128], Wf[s], identv))(pS, s), [bs["Wf"], B_sm["ident"]], [bpS], sig=False)
            MM(pS[:, 128:256], Wf[s], Qf[s], True, True, [bs["Wf"], bs["Qf"]], [bpS], sig=True)

        def setup_act_l(g):
            s, s3, bs = slots(g)
            pS, bpS = psS[s]
            ACTF(L1[s], pS[:, 0:128], AF.Copy, [bpS], [bs["L1"]])
            ACTF(L2[s][:, 0:64], pS[:, 64:128], AF.Copy, [bpS], [bs["L2"]])
            ACTF(L2[s][:, 64:128], pS[:, 0:64], AF.Copy, [bpS], [bs["L2"]], scale=-1.0)

        def setup_dve_toe(g):
            s, s3, bs = slots(g)
            pS, bpS = psS[s]
            TT("dve", toet[s], pS[:, 128:256], maskv, ALU.mult, [bpS, B_sm["mask"], bs["L1"], bs["L2"]], [bs["toet"]])
            STT("dve", Toe[s3], identv, sm["Dv"][:, g:g + 1], toet[s], ALU.mult, ALU.add,
                [B_sm["ident"], B_sm["Dv"], bs["toet"]], [bs["Toe"]])

        def tabs(g):
            tsl = tab_slot[g]
            return EC[tsl][:, g % 2, :], ES[tsl][:, g % 2, :], B_EC[tsl], B_ES[tsl]

        def main_a_pe(g):
            s, s3, bs = slots(g)
            Ug = UG[:, g, :]
            MM(psS1[s][0][:, 0:NCH], L1[s], Ug, True, True, [bs["L1"], B_UG[g]], [psS1[s][1]], sig=True)
            MM(psS2[s][0][:, 0:NCH], L2[s], Ug, True, True, [bs["L2"], B_UG[g]], [psS2[s][1]], sig=True)

        def main_a_mod(g):
            s, s3, bs = slots(g)
            ec, es, bec, bes = tabs(g)
            TT("dve", M1[s], psS1[s][0][:, 0:NCH], ec, ALU.mult, [psS1[s][1], bec], [bs["M1"]])
            TT("dve", M2[s], psS2[s][0][:, 0:NCH], es, ALU.mult, [psS2[s][1], bes], [bs["M2"]])
            TT("pool", M1[s], M1[s], M2[s], ALU.add, [bs["M1"], bs["M2"]], [bs["M1"]])

        def main_a_scan(g):
            s, s3, bs = slots(g)
            ec, es, bec, bes = tabs(g)
            P.op("dve", (lambda s, g: lambda e: e.tensor_tensor_scan(
                out=M2[s], data0=t_r8[:, g:g + 1].to_broadcast([128, NCH]), data1=M1[s], initial=0.0,
                op0=ALU.mult, op1=ALU.add))(s, g), [bs["M1"], bt], [bs["M2"]])
            TT("dve", X1b[s][:, 1:NCH], M2[s][:, 0:NCH - 1], ec[:, 0:NCH - 1], ALU.mult, [bs["M2"], bec], [bs["X1"]])
            TT("dve", X2b[s][:, 1:NCH], M2[s][:, 0:NCH - 1], es[:, 0:NCH - 1], ALU.mult, [bs["M2"], bes], [bs["X2"]])

        def main_b(g):
            s, s3, bs = slots(g)
            Ug = UG[:, g, :]
            pY, bpY = psY[s]
            MM(pY[:, 0:NCH], Toe[s3], Ug, True, False, [bs["Toe"], B_UG[g]], [bpY])
            MM(pY[:, 0:NCH], R1[s3], X1b[s][:, 0:NCH], False, False, [bs["R1"], bs["X1"]], [bpY])
            MM(pY[:, 0:NCH], R2[s3], X2b[s][:, 0:NCH], False, True, [bs["R2"], bs["X2"]], [bpY], sig=True)
            ACTF(Ug, pY[:, 0:NCH], AF.Gelu, [bpY], [B_UG[g]])
            if g % 8 == 7:
                j = g // 8
                for tau in range(8):
                    dst = scrG[j, :, tau, :].rearrange("(g h) c -> h g c", h=16)
                    DMA("sp", dst, UG[16 * tau:16 * tau + 16, 8 * j:8 * j + 8, :], B_UG[8 * j:8 * j + 8], [B_scrG[j]], join=(tau > 0))
                DMA("sp", ugT[:, j, :, :], scrG[j], [B_scrG[j]], B_ug[8 * j:8 * j + 8])

        tables_act1(0)
        tables_dve(0)
        tables_act(0)
        for step in range(32 + 2):
            g0, g1, g2 = step, step - 1, step - 2
            v0, v1, v2 = g0 < 32, 0 <= g1 < 32, 0 <= g2 < 32
            gen1 = (step % 2 == 1) and (step + 1 < 32)
            gen2 = (step % 2 == 0) and (2 <= step < 32)
            if v2:
                main_b(g2)
            if v1:
                main_a_pe(g1)
            if v0:
                setup_dve_wq(g0)
                setup_pool_r(g0)
                setup_pe(g0)
            if v1:
                main_a_mod(g1)
            if v0:
                setup_act_l(g0)
            if gen1:
                tables_act1(step + 1)
            if gen2:
                tables_act(step)
            if v1:
                setup_dve_toe(g1)
                main_a_scan(g1)
            if gen1:
                tables_dve(step + 1)

        if dbg:
            for j in range(4):
                DMA("sp", dbg_d["gT"][:, j * 8 * NCH:(j + 1) * 8 * NCH], ugT[:, j, :, :].rearrange("p k c -> p (k c)"),
                    B_ug[8 * j:8 * j + 8], [])

        P.barrier()
        B_wup = [Buf(), Buf(), Buf()]
        DMA("pool", wup[:, 0, :, :], w_up_d[0], [], [B_wup[0]])
        DMA("pool", wup[:, 1, :, :], w_up_d[1], [], [B_wup[1]])

        ps_rot = Rot([(banks[i], bbuf[i]) for i in range(4)])
        ps_acc = Rot([(banks[4], bbuf[4]), (banks[5], bbuf[5])])
        ps_stat = (banks[6], bbuf[6])
        B_hn = [Buf(f"hn{i}") for i in range(5)]
        B_sqh, B_rstd2, B_sqrt2 = Buf(), Buf(), Buf()
        P.op("pool", lambda e: e.memset(hnT[:, :, 0:2], 0.0), [], [B_hn[0]])
        cob2 = [cobf, cobf_b]
        B_cob2 = [B_cobf, Buf()]
        sg2 = [sgb, tcb2[0]]
        B_sg2 = [B_sg, B_tc2[0]]

        def m2_z(ti, o):
            s0, V = TILES[ti]
            c0, c1 = s0 // 8, (s0 + V) // 8
            gview = lambda k: ugT[:, k, :, c0:c1].rearrange("p k c -> p c k")
            ps, bps = ps_rot.get()
            for k in range(4):
                MM(ps[:, 0:V], w_glu[:, k, 128 * o:128 * (o + 1)], gview(k), k == 0, k == 3, [B_wglu] + B_ug[8 * k:8 * k + 8], [bps], sig=(k == 3))
            sg, bsg = sg2[o % 2], B_sg2[o % 2]
            ACTF(sg[:, 0:V], ps[:, 0:V], AF.Sigmoid, [bps], [bsg])
            TT("dve", co[:, o, 0:V].rearrange("p (c k) -> p c k", k=8), gview(o), sg[:, 0:V].rearrange("p (c k) -> p c k", k=8),
               ALU.mult, B_ug[8 * o:8 * o + 8] + [bsg], [B_co])

        def m2_a_stats(ti):
            s0, V = TILES[ti]
            ACTF(sqb[:, 0:4, 0:V], co[:, :, 0:V], AF.Square, [B_co], [B_sqb])
            rms_stats(V, 4, lambda k: sqb[:, k, 0:V], B_sqb, 1.0 / 512, sqrt_t, sqrt_t, B_sqrt, B_sqrt)

        def m2_a_cobf(ti):
            s0, V = TILES[ti]
            cb, bcb = cob2[ti % 2], B_cob2[ti % 2]
            for j in range(4):
                STT("dve", cb[:, j, 0:V], co[:, j, 0:V], sm["gains"][:, j:j + 1], sqrt_t[:, 0:V],
                    ALU.mult, ALU.mult, [B_co, B_sm["gains"], B_sqrt], [bcb])

        def m2_wout(ti, o):
            s0, V = TILES[ti]
            bh = B_h[ti]
            cb, bcb = cob2[ti % 2], B_cob2[ti % 2]
            ps, bps = ps_acc.get()
            for k in range(4):
                MM(ps[:, 0:V], w_outh[:, k, 128 * o:128 * (o + 1)], cb[:, k, 0:V], k == 0, k == 3,
                   [B_wouth, bcb], [bps], sig=(k == 3))
            TT("dve", hT[:, o, s0 + 2:s0 + 2 + V], hT[:, o, s0 + 2:s0 + 2 + V], ps[:, 0:V], ALU.add, [bps, bh], [bh])

        def m2_b2sq(ti):
            s0, V = TILES[ti]
            bh = B_h[ti]
            ACTF(sqb[:, :, 0:V], hT[:, :, s0 + 2:s0 + 2 + V], AF.Square, [bh], [B_sqb])

        def m2_b2mm(ti):
            s0, V = TILES[ti]
            rms_stats(V, KD, lambda k: sqb[:, k, 0:V], B_sqb, 1.0 / D, rstd, rstd, B_rstd, B_rstd)

        def m2_b2s(ti):
            m2_b2sq(ti)
            m2_b2mm(ti)

        def m2_b2x(ti):
            s0, V = TILES[ti]
            bh = B_h[ti]
            for k in range(KD):
                STT("dve", hnT[:, k, s0 + 2:s0 + 2 + V], hT[:, k, s0 + 2:s0 + 2 + V], gffn[:, k:k + 1],
                    rstd[:, 0:V], ALU.mult, ALU.mult, [bh, B_sm["gffn"], B_rstd], [B_hn[ti]])

        for o in range(4):
            m2_z(0, o)
        m2_a_stats(0)
        m2_a_cobf(0)
        for ti in range(5):
            nxt = ti + 1 < 5
            for o in range(4):
                if nxt:
                    m2_z(ti + 1, o)
                m2_wout(ti, 2 * o)
                m2_wout(ti, 2 * o + 1)
                if o == 0 and ti > 0:
                    m2_b2sq(ti - 1)
                if o == 1 and ti > 0:
                    m2_b2mm(ti - 1)
                if o == 3 and ti > 0:
                    m2_b2x(ti - 1)
            if nxt:
                m2_a_stats(ti + 1)
                m2_a_cobf(ti + 1)
        m2_b2s(4)
        m2_b2x(4)

        if dbg:
            for k in range(KD):
                DMA("sp", dbg_d["h_mix"][:, k, :], hT[:, k, :], B_h, [])

        P.barrier()

        B_wdn = [Buf() for _ in range(10)]
        B_hid = [Buf() for _ in range(10)]
        B_a0, B_v0 = [Buf(), Buf(), Buf()], [Buf(), Buf(), Buf()]
        B_sa = [Buf(), Buf()]
        if OPT_FFN:
            psA = Rot([(banks[0], bbuf[0]), (banks[1], bbuf[1]), (banks[2], bbuf[2])])
            psB = Rot([(banks[3], bbuf[3]), (banks[4], bbuf[4]), (banks[5], bbuf[5])])
            psD = Rot([(banks[6], bbuf[6]), (banks[7], bbuf[7])])
        else:
            psA = Rot([(banks[0], bbuf[0]), (banks[1], bbuf[1])])
            psB = Rot([(banks[2], bbuf[2]), (banks[3], bbuf[3])])
            psD = Rot([(banks[4], bbuf[4]), (banks[5], bbuf[5])])
        fwa = sm["fwa"][:].rearrange("p (t f) -> p t f", t=3)
        fwv = sm["fwv"][:].rearrange("p (t f) -> p t f", t=3)
        fba, fbv = sm["fba"], sm["fbv"]
        rr = [0]

        def load_wup(f):
            DMA("pool", wup[:, f % 3, :, :], w_up_d[f], [], [B_wup[f % 3]])

        def load_wdn(f):
            DMA("pool", wdn[f % 10], w_dn_d[f], [], [B_wdn[f % 10]])

        def up_tile(f):
            ws = f % 3
            hs = f % 10
            for ti, (s0, V) in enumerate(TILES):
                N = V + 2
                bhn = [B_hn[ti]] + ([B_hn[ti - 1]] if ti > 0 else [])
                pa, bpa = psA.get()
                pb, bpb = psB.get()
                for k in range(KD):
                    MM(pa[:, 0:N], wup[:, ws, k, 0:128], hnT[:, k, s0:s0 + N], k == 0, k == KD - 1, [B_wup[ws]] + bhn, [bpa], sig=(k == KD - 1))
                for k in range(KD):
                    MM(pb[:, 0:N], wup[:, ws, k, 128:256], hnT[:, k, s0:s0 + N], k == 0, k == KD - 1, [B_wup[ws]] + bhn, [bpb], sig=(k == KD - 1))
                q = rr[0] % (3 if OPT_FFN else 2)
                rr[0] += 1
                a0, v0 = a0b[q], v0b[q]
                if OPT_FFN2:
                    ACTF(a0[:, 0:V], pa[:, 2:N], AF.Identity, [bpa, B_sm["fwa"], B_sm["fba"]], [B_a0[q]], bias=fba[:, f:f + 1], scale=fwa[:, 2, f:f + 1])
                    ACTF(v0[:, 0:V], pb[:, 2:N], AF.Identity, [bpb, B_sm["fwv"], B_sm["fbv"]], [B_v0[q]], bias=fbv[:, f:f + 1], scale=fwv[:, 2, f:f + 1])
                    STT("dve", a0[:, 0:V], pa[:, 1:N - 1], fwa[:, 1, f:f + 1], a0[:, 0:V], ALU.mult, ALU.add, [bpa, B_a0[q], B_sm["fwa"]], [B_a0[q]])
                    STT("dve", v0[:, 0:V], pb[:, 1:N - 1], fwv[:, 1, f:f + 1], v0[:, 0:V], ALU.mult, ALU.add, [bpb, B_v0[q], B_sm["fwv"]], [B_v0[q]])
                    STT("dve", a0[:, 0:V], pa[:, 0:N - 2], fwa[:, 0, f:f + 1], a0[:, 0:V], ALU.mult, ALU.add, [bpa, B_a0[q], B_sm["fwa"]], [B_a0[q]])
                    STT("dve", v0[:, 0:V], pb[:, 0:N - 2], fwv[:, 0, f:f + 1], v0[:, 0:V], ALU.mult, ALU.add, [bpb, B_v0[q], B_sm["fwv"]], [B_v0[q]])
                    ACTF(a0[:, 0:V], a0[:, 0:V], AF.Silu, [B_a0[q]], [B_a0[q]])
                    TT("pool", hid[hs][:, s0:s0 + V], a0[:, 0:V], v0[:, 0:V], ALU.mult, [B_a0[q], B_v0[q]], [B_hid[hs]])
                else:
                    sa = sab[q]
                    ACTF(a0[:, 0:V], pa[:, 2:N], AF.Identity, [bpa, B_sm["fwa"], B_sm["fba"]], [B_a0[q]], bias=fba[:, f:f + 1], scale=fwa[:, 2, f:f + 1])
                    STT("dve", a0[:, 0:V], pa[:, 1:N - 1], fwa[:, 1, f:f + 1], a0[:, 0:V], ALU.mult, ALU.add, [bpa, B_a0[q], B_sm["fwa"]], [B_a0[q]])
                    STT("dve", a0[:, 0:V], pa[:, 0:N - 2], fwa[:, 0, f:f + 1], a0[:, 0:V], ALU.mult, ALU.add, [bpa, B_a0[q], B_sm["fwa"]], [B_a0[q]])
                    ACTF(sa[:, 0:V], a0[:, 0:V], AF.Silu, [B_a0[q]], [B_sa[q]])
                    ACTF(v0[:, 0:V], pb[:, 2:N], AF.Identity, [bpb, B_sm["fwv"], B_sm["fbv"]], [B_v0[q]], bias=fbv[:, f:f + 1], scale=fwv[:, 2, f:f + 1])
                    STT("dve", v0[:, 0:V], pb[:, 1:N - 1], fwv[:, 1, f:f + 1], v0[:, 0:V], ALU.mult, ALU.add, [bpb, B_v0[q], B_sm["fwv"]], [B_v0[q]])
                    STT("dve", v0[:, 0:V], pb[:, 0:N - 2], fwv[:, 0, f:f + 1], v0[:, 0:V], ALU.mult, ALU.add, [bpb, B_v0[q], B_sm["fwv"]], [B_v0[q]])
                    TT("pool", hid[hs][:, s0:s0 + V], sa[:, 0:V], v0[:, 0:V], ALU.mult, [B_sa[q], B_v0[q]], [B_hid[hs]])

        def final_stats(ti):
            s0, V = TILES[ti]
            bh = B_h[ti]
            ACTF(sqh[:, :, 0:V], hT[:, :, s0 + 2:s0 + 2 + V], AF.Square, [bh], [B_sqh])
            rms_stats(V, KD, lambda k: sqh[:, k, 0:V], B_sqh, 1.0 / D, rstd2, rstd2, B_rstd2, B_rstd2, pss=(banks[ti % 3], bbuf[ti % 3]))

        def final_scale(ti, k):
            s0, V = TILES[ti]
            bh = B_h[ti]
            STT("dve", hT[:, k, s0 + 2:s0 + 2 + V], hT[:, k, s0 + 2:s0 + 2 + V], gfin[:, k:k + 1],
                rstd2[:, 0:V], ALU.mult, ALU.mult, [bh, B_sm["gfin"], B_rstd2], [bh])

        def final_store(ti):
            s0, V = TILES[ti]
            t0 = max(s0, NMETA)
            outs.append(DMA("sp", yT_d[:, :, t0 - NMETA:s0 + V - NMETA], hT[:, :, t0 + 2:s0 + 2 + V], [B_h[ti]], []))

        def down_group(fs, last=False):
            for ti, (s0, V) in enumerate(TILES):
                nk = 0
                for o in range(KD):
                    ps, bps = psD.get()
                    for i, f in enumerate(fs):
                        MM(ps[:, 0:V], wdn[f % 10][:, 128 * o:128 * (o + 1)], hid[f % 10][:, s0:s0 + V], i == 0, i == len(fs) - 1,
                           [B_wdn[f % 10], B_hid[f % 10]], [bps], sig=(i == len(fs) - 1))
                    TT("dve", hT[:, o, s0 + 2:s0 + 2 + V], hT[:, o, s0 + 2:s0 + 2 + V], ps[:, 0:V], ALU.add, [bps, B_h[ti]], [B_h[ti]])
                    if last and ti >= 1:
                        if o == 1:
                            final_stats(ti - 1)
                        if o >= 2:
                            final_scale(ti - 1, nk)
                            nk += 1
                if last and ti >= 1:
                    while nk < KD:
                        final_scale(ti - 1, nk)
                        nk += 1
                    final_store(ti - 1)
            if last:
                ti = len(TILES) - 1
                final_stats(ti)
                for k in range(KD):
                    final_scale(ti, k)
                final_store(ti)

        outs = []
        load_wdn(0)
        for gi, fs in enumerate(FGROUPS):
            for f in fs:
                if f + 2 < NF:
                    load_wup(f + 2)
                if f + 1 < NF:
                    load_wdn(f + 1)
                up_tile(f)
                if gi > 0 and f == fs[0]:
                    down_group(FGROUPS[gi - 1])
        down_group(FGROUPS[-1], last=True)

        fin = P.op("sp", lambda e: None, sig=False)
        fin.deps.extend(P.all_dmas)
        fin.deps.extend(outs)
        P.emit(block)
    return nc


def _kmajor(w):
    K = w.shape[0] // 128
    return np.ascontiguousarray(w.reshape(K, 128, w.shape[1]).transpose(1, 0, 2))


def _percol(v, K):
    return np.ascontiguousarray(v.reshape(K, 128).T)


def _prep_shared(inp):
    f = lambda a: np.asarray(a, dtype=np.float32)
    sh = {}
    sh["w_in"] = _kmajor(f(inp["w_in"])[0])
    sh["w_out"] = _kmajor(f(inp["w_out"])[0])
    sh["w_glu"] = _kmajor(f(inp["ssm_w_glu"])[0])
    wu = _kmajor(f(inp["w_up"])[0])
    wa = wu[:, :, :DFF].reshape(128, KD, NF, 128)
    wv = wu[:, :, DFF:].reshape(128, KD, NF, 128)
    sh["w_up"] = np.ascontiguousarray(np.concatenate([wa, wv], axis=3).transpose(2, 0, 1, 3))
    sh["w_dn"] = np.ascontiguousarray(f(inp["w_down"])[0].reshape(NF, 128, 1024))
    sh["gmix"] = _percol(f(inp["norm_mix_g"])[0], 8)
    sh["gffn"] = _percol(f(inp["norm_ffn_g"])[0], 8)
    sh["gfin"] = _percol(f(inp["norm_final_g"]), 8)
    sh["gainc"] = _percol(f(inp["gain_conv_out"])[0], 4)
    sh["gains"] = _percol(f(inp["gain_ssm_out"])[0], 4)
    cw = f(inp["conv_w"])[0]
    sh["cw"] = np.ascontiguousarray(cw.reshape(3, 4, 128).transpose(2, 0, 1).reshape(128, 12))
    fw = f(inp["ffn_conv_w"])[0]
    sh["fwa"] = np.ascontiguousarray(fw[:, :DFF].reshape(3, NF, 128).transpose(2, 0, 1).reshape(128, 66))
    sh["fwv"] = np.ascontiguousarray(fw[:, DFF:].reshape(3, NF, 128).transpose(2, 0, 1).reshape(128, 66))
    fb = f(inp["ffn_conv_b"])[0]
    sh["fba"] = np.ascontiguousarray(fb[:DFF].reshape(NF, 128).T)
    sh["fbv"] = np.ascontiguousarray(fb[DFF:].reshape(NF, 128).T)
    dup = lambda a: np.ascontiguousarray(np.concatenate([a, a], axis=0))
    stack = lambda a, b: np.ascontiguousarray(np.concatenate([a, b], axis=0))
    sh["lr"] = dup(f(inp["ssm_lam_re"])[0].T)
    sh["li"] = dup(f(inp["ssm_lam_im"])[0].T)
    sh["ldt"] = np.ascontiguousarray(np.broadcast_to(f(inp["ssm_log_dt"])[0][None, :], (128, 32)))
    bre = f(inp["ssm_b_re"])[0].transpose(1, 0, 2).reshape(64, 512)
    bim = f(inp["ssm_b_im"])[0].transpose(1, 0, 2).reshape(64, 512)
    sh["X1"] = stack(bre, bim)
    sh["X2"] = stack(bim, bre)
    cre = f(inp["ssm_c_re"])[0].transpose(2, 0, 1).reshape(64, 512)
    cim = f(inp["ssm_c_im"])[0].transpose(2, 0, 1).reshape(64, 512)
    sh["CX1"] = stack(cre, cim)
    sh["CX2"] = stack(cim, cre)
    dd = f(inp["ssm_d"])[0]
    sh["Dv"] = np.ascontiguousarray(np.tile(dd.T, (8, 1)))
    sh["ident"] = np.eye(128, dtype=np.float32)
    kk = np.arange(128) // 16
    sh["mask"] = (kk[None, :] >= kk[:, None]).astype(np.float32)
    sh["cidx"] = np.ascontiguousarray(np.broadcast_to(np.arange(NCH, dtype=np.float32)[None, :], (128, NCH)))
    sh["jv"] = np.ascontiguousarray(np.broadcast_to(np.array(JV, dtype=np.float32)[None, :], (128, 24)))
    sh["sgn"] = np.where(np.arange(128) < 64, -1.0, 1.0).astype(np.float32)[:, None]
    return sh


def _prep_x(x_b, meta):
    h0 = np.concatenate([meta, x_b], axis=0)
    hT = h0.T.reshape(KD, 128, L).transpose(1, 0, 2)
    out = np.zeros((128, KD, LP), dtype=np.float32)
    out[:, :, 2:] = hT
    return out


_NC_CACHE = {}


def kernel(**inputs):
    x = np.asarray(inputs["x"], dtype=np.float32)
    meta = np.asarray(inputs["meta_tokens"], dtype=np.float32)
    sh = _prep_shared(inputs)
    dbg = bool(inputs.get("_dbg", False)) if isinstance(inputs, dict) else False
    if dbg not in _NC_CACHE:
        _NC_CACHE[dbg] = _build(dbg)
    nc = _NC_CACHE[dbg]
    B = x.shape[0]
    in_maps = []
    for b in range(B):
        m = dict(sh)
        m["xT"] = _prep_x(x[b], meta)
        in_maps.append(m)
    res = run_bass_kernel_spmd(nc, in_maps, core_ids=list(range(B)))
    out = np.empty((B, 2048, D), dtype=np.float32)
    for b in range(B):
        yT = np.asarray(res.results[b]["yT"])
        out[b] = yT.transpose(2, 1, 0).reshape(2048, D)
    if dbg:
        kernel.last = res
    return out
```
